# Optimizing a Trainium2 kernel written in Bass

```python
import math, functools
import jax, jax.numpy as jnp
from jax import lax
import numpy as np

D_MODEL = 1024
BATCH = 2
SEQ = 8192
DEPTH = 1
DEC_BATCH = 128
DEC_SEQ = 4
PAST_LEN = 16384
PAGE_SIZE = 128

HEAD_DIM = 64
N_HEADS = D_MODEL // HEAD_DIM
N_KV_HEADS = 4
GQA_GROUP = N_HEADS // N_KV_HEADS
WINDOW = 128
BLOCK = WINDOW
D_ATTN = N_HEADS * HEAD_DIM
D_KV = N_KV_HEADS * HEAD_DIM
D_SSM = D_MODEL
SSM_CH = 16
SSM_GROUPS = D_SSM // SSM_CH
SSM_STATE = 64
D_FF = 4 * D_MODEL
RMS_EPS = 1e-5
DT_MIN = 0.001
DT_MAX = 0.1
D_IN = D_ATTN + 2 * D_KV + D_SSM + 2 * D_MODEL

kernel_name = "hybrid_swa_sink_s5_decoder_step"


def rmsnorm(x, g):
    xf = x.astype(jnp.float32)
    y = xf * lax.rsqrt(jnp.mean(xf * xf, axis=-1, keepdims=True) + RMS_EPS)
    return (y * g.astype(jnp.float32)).astype(x.dtype)


def sink_attention(q, k, v, valid, sinks):
    s = jnp.einsum('...qkgd,...skd->...kgqs', q, k,
                   preferred_element_type=jnp.float32) * (HEAD_DIM ** -0.5)
    s = jnp.where(valid, s, -jnp.inf)
    sink = sinks.astype(jnp.float32).reshape(N_KV_HEADS, GQA_GROUP, 1, 1)
    m = jnp.maximum(jnp.max(s, axis=-1, keepdims=True), sink)
    p = jnp.exp(s - m)
    denom = jnp.sum(p, axis=-1, keepdims=True) + jnp.exp(sink - m)
    return jnp.einsum('...kgqs,...skd->...qkgd', (p / denom).astype(v.dtype), v)


def prompt_attention(q, k, v, sinks):
    b, l = q.shape[0], q.shape[1]
    nb = l // BLOCK
    qb = q.reshape(b, nb, BLOCK, N_KV_HEADS, GQA_GROUP, HEAD_DIM)
    kb = k.reshape(b, nb, BLOCK, N_KV_HEADS, HEAD_DIM)
    vb = v.reshape(b, nb, BLOCK, N_KV_HEADS, HEAD_DIM)
    pad = jnp.zeros_like(kb[:, :1])
    k2 = jnp.concatenate([jnp.concatenate([pad, kb[:, :-1]], axis=1), kb], axis=2)
    v2 = jnp.concatenate([jnp.concatenate([pad, vb[:, :-1]], axis=1), vb], axis=2)
    qi = jnp.arange(BLOCK)[:, None]
    si = jnp.arange(2 * BLOCK)[None, :]
    diff = qi + BLOCK - si
    band = (diff >= 0) & (diff < WINDOW)
    has_prev = (jnp.arange(nb) > 0)[:, None, None] | (si >= BLOCK)[None]
    valid = (band[None] & has_prev)[:, None, None]
    o = sink_attention(qb, k2, v2, valid, sinks)
    return (o.reshape(b, l, D_ATTN), k[:, -WINDOW:], v[:, -WINDOW:])


def sample_attention(q, k, v, sinks, cache_k, cache_v):
    b, t = q.shape[0], q.shape[1]
    w = cache_k.shape[1]
    kk = jnp.concatenate([cache_k, k], axis=1)
    vv = jnp.concatenate([cache_v, v], axis=1)
    qpos = PAST_LEN + jnp.arange(t)
    kpos = PAST_LEN - w + jnp.arange(w + t)
    diff = qpos[:, None] - kpos[None, :]
    valid = (diff >= 0) & (diff < WINDOW)
    qg = q.reshape(b, t, N_KV_HEADS, GQA_GROUP, HEAD_DIM)
    o = sink_attention(qg, kk, vv, valid, sinks)
    return (o.reshape(b, t, D_ATTN), kk[:, -WINDOW:], vv[:, -WINDOW:])


def _cplx_affine_combine(e1, e2):
    a1r, a1i, b1r, b1i = e1
    a2r, a2i, b2r, b2i = e2
    return (a2r * a1r - a2i * a1i,
            a2r * a1i + a2i * a1r,
            a2r * b1r - a2i * b1i + b2r,
            a2r * b1i + a2i * b1r + b2i)


def s5_discretize(lam_re, lam_im, log_dt, b_re, b_im):
    dt = jnp.exp(log_dt)[:, None]
    decay = jnp.exp(lam_re * dt)
    ab_re = decay * jnp.cos(lam_im * dt)
    ab_im = decay * jnp.sin(lam_im * dt)
    nr, ni = ab_re - 1.0, ab_im
    den = lam_re * lam_re + lam_im * lam_im
    f_re = ((nr * lam_re + ni * lam_im) / den)[..., None]
    f_im = ((ni * lam_re - nr * lam_im) / den)[..., None]
    bb_re = f_re * b_re - f_im * b_im
    bb_im = f_re * b_im + f_im * b_re
    return ab_re, ab_im, bb_re, bb_im


def s5_branch(u, h0_re, h0_im, lam_re, lam_im, log_dt, b_re, b_im, c_re, c_im, d_skip):
    f32 = jnp.float32
    b, l = u.shape[0], u.shape[1]
    uf = u.astype(f32)
    ug = uf.reshape(b, l, SSM_GROUPS, SSM_CH)
    ab_re, ab_im, bb_re, bb_im = s5_discretize(
        lam_re.astype(f32), lam_im.astype(f32), log_dt.astype(f32), b_re.astype(f32), b_im.astype(f32))
    bu_re = jnp.einsum('blgc,gpc->blgp', ug, bb_re)
    bu_im = jnp.einsum('blgc,gpc->blgp', ug, bb_im)
    h0r, h0i = h0_re.astype(f32), h0_im.astype(f32)
    first_re = ab_re * h0r - ab_im * h0i + bu_re[:, 0]
    first_im = ab_re * h0i + ab_im * h0r + bu_im[:, 0]
    bu_re = bu_re.at[:, 0].set(first_re)
    bu_im = bu_im.at[:, 0].set(first_im)
    a_re = jnp.broadcast_to(ab_re, bu_re.shape)
    a_im = jnp.broadcast_to(ab_im, bu_im.shape)
    _, _, h_re, h_im = lax.associative_scan(_cplx_affine_combine, (a_re, a_im, bu_re, bu_im), axis=1)
    y = (jnp.einsum('blgp,gcp->blgc', h_re, c_re.astype(f32))
         - jnp.einsum('blgp,gcp->blgc', h_im, c_im.astype(f32)))
    y = y.reshape(b, l, D_SSM) + d_skip.astype(f32) * uf
    return y.astype(u.dtype), h_re[:, -1], h_im[:, -1]


def hybrid_layer(x, attn_fn, h0_re, h0_im, g_mix, w_in, sinks, w_attn_o, lam_re, lam_im, log_dt,
                 b_re, b_im, c_re, c_im, d_skip, w_glu, w_out, g_ffn, w_up, w_down):
    b, l = x.shape[0], x.shape[1]
    h = rmsnorm(x, g_mix)
    proj = h @ w_in
    q, k, v, u, gate_logits = jnp.split(
        proj, [D_ATTN, D_ATTN + D_KV, D_ATTN + 2 * D_KV, D_ATTN + 2 * D_KV + D_SSM], axis=-1)
    q = q.reshape(b, l, N_HEADS, HEAD_DIM)
    k = k.reshape(b, l, N_KV_HEADS, HEAD_DIM)
    v = v.reshape(b, l, N_KV_HEADS, HEAD_DIM)
    attn, k_buf, v_buf = attn_fn(q, k, v, sinks)
    a_out = attn @ w_attn_o
    y_ssm, hT_re, hT_im = s5_branch(u, h0_re, h0_im, lam_re, lam_im, log_dt,
                                    b_re, b_im, c_re, c_im, d_skip)
    glu = jax.nn.gelu(y_ssm) @ w_glu
    s_out = glu[..., :D_MODEL] * jax.nn.sigmoid(glu[..., D_MODEL:])
    merged = (jax.nn.sigmoid(gate_logits[..., :D_MODEL]) * a_out
              + jax.nn.sigmoid(gate_logits[..., D_MODEL:]) * s_out)
    x = x + merged @ w_out
    h2 = rmsnorm(x, g_ffn)
    x = x + jnp.square(jax.nn.relu(h2 @ w_up)) @ w_down
    return x, k_buf, v_buf, hT_re, hT_im


def setup_inputs(seed: int = 0) -> dict:
    key = jax.random.key(seed)
    ks = jax.random.split(key, 24)
    f32 = jnp.float32
    nrm = lambda k, shape, scale: jax.random.normal(k, shape, f32) * scale
    lam_im_base = jnp.broadcast_to(math.pi * jnp.arange(SSM_STATE, dtype=f32), (DEPTH, SSM_GROUPS, SSM_STATE))
    return {
        "x_prompt": nrm(ks[0], (BATCH, SEQ, D_MODEL), 1.0),
        "x_sample": nrm(ks[1], (DEC_BATCH, DEC_SEQ, D_MODEL), 1.0),
        "cache_k": nrm(ks[2], (DEPTH, DEC_BATCH, WINDOW, N_KV_HEADS, HEAD_DIM), 1.0),
        "cache_v": nrm(ks[3], (DEPTH, DEC_BATCH, WINDOW, N_KV_HEADS, HEAD_DIM), 1.0),
        "state_ssm_re": nrm(ks[4], (DEPTH, DEC_BATCH, SSM_GROUPS, SSM_STATE), 0.5),
        "state_ssm_im": nrm(ks[5], (DEPTH, DEC_BATCH, SSM_GROUPS, SSM_STATE), 0.5),
        "g_mix": 1.0 + nrm(ks[6], (DEPTH, D_MODEL), 0.1),
        "w_in": nrm(ks[7], (DEPTH, D_MODEL, D_IN), D_MODEL ** -0.5),
        "attn_sinks": nrm(ks[8], (DEPTH, N_HEADS), 0.5),
        "w_attn_o": nrm(ks[9], (DEPTH, D_ATTN, D_MODEL), D_ATTN ** -0.5),
        "ssm_lambda_re": -0.5 + nrm(ks[10], (DEPTH, SSM_GROUPS, SSM_STATE), 0.01),
        "ssm_lambda_im": lam_im_base + nrm(ks[11], (DEPTH, SSM_GROUPS, SSM_STATE), 0.01),
        "ssm_log_dt": jax.random.uniform(ks[12], (DEPTH, SSM_GROUPS), f32,
                                         math.log(DT_MIN), math.log(DT_MAX)),
        "ssm_b_re": nrm(ks[13], (DEPTH, SSM_GROUPS, SSM_STATE, SSM_CH), (2 * SSM_CH) ** -0.5),
        "ssm_b_im": nrm(ks[14], (DEPTH, SSM_GROUPS, SSM_STATE, SSM_CH), (2 * SSM_CH) ** -0.5),
        "ssm_c_re": nrm(ks[15], (DEPTH, SSM_GROUPS, SSM_CH, SSM_STATE), (2 * SSM_STATE) ** -0.5),
        "ssm_c_im": nrm(ks[16], (DEPTH, SSM_GROUPS, SSM_CH, SSM_STATE), (2 * SSM_STATE) ** -0.5),
        "ssm_d": nrm(ks[17], (DEPTH, D_SSM), 1.0),
        "w_glu": nrm(ks[18], (DEPTH, D_SSM, 2 * D_MODEL), D_SSM ** -0.5),
        "w_out": nrm(ks[19], (DEPTH, D_MODEL, D_MODEL), D_MODEL ** -0.5),
        "g_ffn": 1.0 + nrm(ks[20], (DEPTH, D_MODEL), 0.1),
        "w_up": nrm(ks[21], (DEPTH, D_MODEL, D_FF), D_MODEL ** -0.5),
        "w_down": nrm(ks[22], (DEPTH, D_FF, D_MODEL), D_FF ** -0.5),
        "g_final": 1.0 + nrm(ks[23], (D_MODEL,), 0.1),
    }


def reference(x_prompt, x_sample, cache_k, cache_v, state_ssm_re, state_ssm_im, g_mix, w_in,
              attn_sinks, w_attn_o, ssm_lambda_re, ssm_lambda_im, ssm_log_dt, ssm_b_re, ssm_b_im,
              ssm_c_re, ssm_c_im, ssm_d, w_glu, w_out, g_ffn, w_up, w_down, g_final):
    yp, ys = x_prompt, x_sample
    kp_l, vp_l, hpr_l, hpi_l = [], [], [], []
    ksl, vsl, hsr_l, hsi_l = [], [], [], []
    h0_zero = jnp.zeros((x_prompt.shape[0], SSM_GROUPS, SSM_STATE), jnp.float32)
    for l in range(DEPTH):
        params = (g_mix[l], w_in[l], attn_sinks[l], w_attn_o[l], ssm_lambda_re[l], ssm_lambda_im[l],
                  ssm_log_dt[l], ssm_b_re[l], ssm_b_im[l], ssm_c_re[l], ssm_c_im[l], ssm_d[l],
                  w_glu[l], w_out[l], g_ffn[l], w_up[l], w_down[l])
        yp, kp, vp, hpr, hpi = hybrid_layer(yp, prompt_attention, h0_zero, h0_zero, *params)
        samp_fn = functools.partial(sample_attention, cache_k=cache_k[l], cache_v=cache_v[l])
        ys, kq, vq, hsr, hsi = hybrid_layer(ys, samp_fn, state_ssm_re[l], state_ssm_im[l], *params)
        kp_l.append(kp); vp_l.append(vp); hpr_l.append(hpr); hpi_l.append(hpi)
        ksl.append(kq); vsl.append(vq); hsr_l.append(hsr); hsi_l.append(hsi)
    y_prompt = rmsnorm(yp, g_final)
    y_sample = rmsnorm(ys, g_final)
    k_prompt = jnp.stack(kp_l, axis=0)
    v_prompt = jnp.stack(vp_l, axis=0)
    ssm_re_prompt = jnp.stack(hpr_l, axis=0)
    ssm_im_prompt = jnp.stack(hpi_l, axis=0)
    k_sample = jnp.stack(ksl, axis=0)
    v_sample = jnp.stack(vsl, axis=0)
    ssm_re_sample = jnp.stack(hsr_l, axis=0)
    ssm_im_sample = jnp.stack(hsi_l, axis=0)
    return (y_prompt, y_sample, k_prompt, v_prompt, ssm_re_prompt, ssm_im_prompt,
            k_sample, v_sample, ssm_re_sample, ssm_im_sample)
```

```python
import numpy as np
from contextlib import ExitStack
import concourse.bass as bass
import concourse.mybir as mybir
from concourse.bass_utils import run_bass_kernel_spmd

F32 = mybir.dt.float32
BF16 = mybir.dt.bfloat16
AF = mybir.ActivationFunctionType
ALU = mybir.AluOpType
AX = mybir.AxisListType

D = 1024
D_IN = 4608
D_FF = 4096
NCORES = 8
TP = 2048
TS = 64
EPS = 1e-5
NEG = -30000.0
ENABLE_SSM = True
import os
NO_ATTN = bool(int(os.environ.get("NO_ATTN", "0")))
DENSE_INC = bool(int(os.environ.get("DENSE_INC", "1")))
PIPE = os.environ.get("PIPE", "ABCDEFGH")
RELAYOUT4 = bool(int(os.environ.get("RELAYOUT4", "1")))
WCONV = bool(int(os.environ.get("WCONV", "0")))


class Sync:
    def __init__(self, nc, es):
        self.nc = nc
        self.eng = {"pe": nc.tensor, "act": nc.scalar, "dve": nc.vector, "pool": nc.gpsimd, "sp": nc.sync}
        self.sem = {k: es.enter_context(nc.semaphore("s_" + k)) for k in ("pe", "act", "dve")}
        self.cnt = {k: 0 for k in self.sem}
        self.seen = {}
        self.wr = {}
        self.rd = {}
        self.es = es
        self.dma_sems = {}
        self.pipe = False

    def _wait(self, e, tok):
        s, v = tok
        key = (e, id(s))
        if self.seen.get(key, 0) >= v:
            return
        if e in self.sem and s is self.sem[e] and ((e == "pe" and self.pipe) or v > self.cnt[e]):
            return
        self.eng[e].wait_ge(s, v)
        self.seen[key] = v

    def deps(self, e, reads, writes):
        for b in reads:
            for t in self.wr.get(b, {}).values():
                self._wait(e, t)
        for b in writes:
            for t in self.wr.get(b, {}).values():
                self._wait(e, t)
            for t in self.rd.get(b, {}).values():
                self._wait(e, t)

    def done(self, tok, reads, writes):
        for b in reads:
            self.rd.setdefault(b, {})[id(tok[0])] = tok
        for b in writes:
            self.wr[b] = {id(tok[0]): tok}
            self.rd[b] = {}

    def op(self, e, fn, reads=(), writes=(), inc=True, mode="full"):
        self.deps(e, reads, writes)
        if e == "pe" and mode != getattr(self, "pe_mode", "full"):
            if self.cnt["pe"] > 0:
                self.eng["pe"].wait_ge(self.sem["pe"], self.cnt["pe"])
            self.pe_mode = mode
        ins = fn(self.eng[e])
        if inc or DENSE_INC:
            self.cnt[e] += 1
            ins.then_inc(self.sem[e], 1)
            self.done((self.sem[e], self.cnt[e]), reads, writes)
        else:
            self.done((self.sem[e], self.cnt[e] + 1), reads, writes)

    def dma(self, q, out, in_, reads=(), writes=(), sem_key=None):
        key = sem_key if sem_key is not None else (writes[0] if writes else reads[0])
        if key not in self.dma_sems:
            self.dma_sems[key] = [self.es.enter_context(self.nc.semaphore("d_%d" % len(self.dma_sems))), 0]
        self.deps(q, reads, writes)
        rec = self.dma_sems[key]
        rec[1] += 16
        self.eng[q].dma_start(out=out, in_=in_).then_inc(rec[0], 16)
        self.done((rec[0], rec[1]), reads, writes)

    def barrier(self):
        toks = [(self.sem[k], self.cnt[k]) for k in self.sem if self.cnt[k] > 0]
        toks += [(s, v) for (s, v) in self.dma_sems.values() if v > 0]
        for e in self.eng:
            for t in toks:
                self._wait(e, t)

    def finish(self):
        for key, (s, v) in self.dma_sems.items():
            if v:
                self.nc.sync.wait_ge(s, v)


NPRE = int(os.environ.get("NPRE", "12"))
TWO_PI = 6.283185307179586


def build_nc():
    nc = bass.Bass("TRN2", target_bir_lowering=False)
    dt_in = lambda n, s: nc.dram_tensor(n, s, F32, kind="ExternalInput").ap()
    dt_out = lambda n, s: nc.dram_tensor(n, s, F32, kind="ExternalOutput").ap()
    xp = dt_in("xp", [TP, D])
    xh = dt_in("xh", [128, D])
    xpre = dt_in("xpre", [max(NPRE, 1) * 512, D])
    xs = dt_in("xs", [TS, D])
    ck = dt_in("ck", [16, 128, 256])
    cv = dt_in("cv", [16, 128, 256])
    g_mix = dt_in("g_mix", [1, D])
    g_ffn = dt_in("g_ffn", [1, D])
    g_fin = dt_in("g_fin", [1, D])
    sinks = dt_in("sinks", [1, 16])
    mask_a = dt_in("mask_a", [128, 256])
    mask_0 = dt_in("mask_0", [128, 256])
    w_in = dt_in("w_in", [D, D_IN])
    w_ao = dt_in("w_ao", [D, D])
    w_gl = dt_in("w_gl", [D, 2 * D])
    w_o = dt_in("w_o", [D, D])
    w_up = dt_in("w_up", [D, D_FF])
    w_down = dt_in("w_down", [D_FF, D])
    lamT_re = dt_in("lamT_re", [128, 64]); lamT_im = dt_in("lamT_im", [128, 64]); ldt_in = dt_in("ldt", [128, 64])
    bT_re = dt_in("bT_re", [128, 64, 16]); bT_im = dt_in("bT_im", [128, 64, 16])
    cT_re = dt_in("cT_re", [128, 64, 16]); cT_im = dt_in("cT_im", [128, 64, 16])
    dT_in = dt_in("dT", [128, 8])
    dd_in = dt_in("dd", [128, 64])
    kvec_in = dt_in("kvec", [128, 41])
    selc = dt_in("selc", [128, 64, 128]); selTc = dt_in("selTc", [128, 64, 128]); bmask_in = dt_in("bmask", [128, 128])
    h0hh_in = dt_in("h0hh", [128, 16, 64]); h0hs_in = dt_in("h0hs", [128, 16, 64])
    msc_in = dt_in("mask_sc", [16, 128]); msn_in = dt_in("mask_sn", [16, 4]); sink_s_in = dt_in("sink_sx", [16, 4])
    yp = dt_out("yp", [TP, D])
    ys = dt_out("ys", [TS, D])
    kp = dt_out("kp", [128, 256])
    vp = dt_out("vp", [128, 256])
    ks = dt_out("ks", [16, 128, 256])
    vs = dt_out("vs", [16, 128, 256])
    st_p = dt_out("st_p", [64, 128])
    st_s = dt_out("st_s", [1024, 128])
    tb = lambda n: nc.dram_tensor(n, [128, 64, 128], BF16).ap()
    RT_d, RTs_d, Toep_d, Om_d = tb("RT_d"), tb("RTs_d"), tb("Toep_d"), tb("Om_d")

    kview = lambda w: w.rearrange("(k p) m -> p k m", p=128)
    wbf = {n: nc.dram_tensor(n + "_bf", list(shp), BF16).ap() for n, shp in
           (("w_in", (D, D_IN)), ("w_ao", (D, D)), ("w_gl", (D, 2 * D)), ("w_o", (D, D)), ("w_up", (D, D_FF)), ("w_down", (D_FF, D)))}
    w_in_v, w_ao_v, w_gl_v, w_o_v, w_up_v, w_down_v = (kview(w_in), kview(w_ao), kview(w_gl), kview(w_o),
                                                      kview(w_up), kview(w_down))
    BFV = {id_: (kview(wbf[n]), "wb_" + n) for id_, n in ((0, "w_in"), (1, "w_ao"), (2, "w_gl"), (3, "w_o"), (4, "w_up"), (5, "w_down"))}
    SRCV = {0: w_in_v, 1: w_ao_v, 2: w_gl_v, 3: w_o_v, 4: w_up_v, 5: w_down_v}

    with ExitStack() as es:
        S = Sync(nc, es)
        sb = lambda n, s, d: es.enter_context(nc.sbuf_tensor(n, s, d))
        ident = sb("ident", [128, 128], BF16)
        psw = sb("psw", [128, 128], BF16)
        identf = sb("identf", [128, 128], F32)
        gm = sb("gm", [128, D], F32)
        gf = sb("gf", [128, D], F32)
        gl = sb("gl", [128, D], F32)
        sink_bc = sb("sink_bc", [128, 16], F32)
        nsink_bc = sb("nsink_bc", [128, 16], F32)
        mA_t = sb("mA_t", [128, 256], F32)
        m0_t = sb("m0_t", [128, 256], F32)
        AR2 = sb("AR2", [128, 2, 64], F32)
        AI2 = sb("AI2", [128, 2, 64], F32)
        PR2 = sb("PR2", [128, 2, 64], F32)
        TR = sb("TR", [128, 7, 64], F32)
        TI = sb("TI", [128, 7, 2, 64], F32)
        PI2 = sb("PI2", [128, 2, 64], F32)
        Dt = sb("Dt", [128, 8], F32)
        X = sb("X", [128, 3, 64], F32)
        t1 = sb("t1", [128, 2, 64], F32)
        t2 = sb("t2", [128, 2, 64], F32)
        ps_t = [es.enter_context(nc.psum_tensor("ps_t%d" % i, [128, 8, 128], BF16)) for i in range(2)]
        ps_m = [es.enter_context(nc.psum_tensor("ps_m%d" % i, [128, 512], F32)) for i in range(6)]

        wctr = [0]

        def next_slot():
            i = wctr[0] % 2
            wctr[0] += 1
            return wslot[i], "wslot%d" % i

        def load_w(view, nk, c0, ncols, dup_heads=False, k0=0, bf=True):
            slot, name = next_slot()
            rd = []
            if WCONV and bf:
                wid = [k_ for k_, v_ in SRCV.items() if v_ is view][0]
                view, rdn = BFV[wid]
                rd = [rdn]
            if dup_heads:
                dst = slot[:, 0:nk * 512].rearrange("p (k h u d) -> p k h u d", k=nk, h=4, u=2)
                for h in range(4):
                    for u in range(2):
                        S.dma("pool", dst[:, :, h, u, :], view[:, 0:nk, c0 + h * 64:c0 + (h + 1) * 64], reads=rd, writes=[name], sem_key=name)
                return slot[:, 0:nk * 512].rearrange("p (k m) -> p k m", k=nk), name
            dst = slot[:, 0:nk * ncols].rearrange("p (k m) -> p k m", k=nk)
            S.dma("pool", dst, view[:, k0:k0 + nk, c0:c0 + ncols], reads=rd, writes=[name], sem_key=name)
            return dst, name

        def load_tab(src):
            slot, name = next_slot()
            dst = slot[:, :].rearrange("p (g m) -> p g m", g=64)
            for hh_ in range(2):
                S.dma("pool", dst[:, 32 * hh_:32 * hh_ + 32, :], src[:, 32 * hh_:32 * hh_ + 32, :], writes=[name])
            return dst, name

        pctr = [0]

        def next_ps():
            i = pctr[0] % 6
            pctr[0] += 1
            return ps_m[i], "ps_m%d" % i

        tctr = [0]

        def next_pt():
            i = tctr[0] % 2
            tctr[0] += 1
            return ps_t[i], "ps_t%d" % i

        V = lambda fn, r=(), w=(): S.op("dve", fn, reads=r, writes=w)
        A_ = lambda fn, r=(), w=(): S.op("act", fn, reads=r, writes=w)

        V(lambda e: e.memset(identf[:], 1.0), w=["identf"])
        nc.gpsimd.wait_ge(S.sem["dve"], S.cnt["dve"])
        pool_sem = es.enter_context(nc.semaphore("s_pool0"))
        nc.gpsimd.affine_select(out=identf[:], in_=identf[:], pattern=[[-1, 128]], compare_op=ALU.is_equal,
                                fill=0.0, base=0, channel_multiplier=1).then_inc(pool_sem, 1)
        S.wr["identf"] = {id(pool_sem): (pool_sem, 1)}
        V(lambda e: e.tensor_copy(ident[:], identf[:]), r=["identf"], w=["ident"])
        V(lambda e: e.tensor_copy(psw[:, 0:64], identf[:, 64:128]), r=["identf"], w=["psw"])
        V(lambda e: e.tensor_copy(psw[:, 64:128], identf[:, 0:64]), r=["identf"], w=["psw"])
        V(lambda e: e.memset(X[:], 0.0), w=["X"])
        S.dma("sp", gm[:], g_mix.partition_broadcast(128), writes=["gm"])
        S.dma("sp", gf[:], g_ffn.partition_broadcast(128), writes=["gf"])
        S.dma("sp", gl[:], g_fin.partition_broadcast(128), writes=["gl"])
        S.dma("sp", sink_bc[:], sinks.partition_broadcast(128), writes=["sink_bc"])
        S.dma("sp", mA_t[:], mask_a, writes=["mA_t"])
        S.dma("sp", m0_t[:], mask_0, writes=["m0_t"])
        S.dma("sp", Dt[:], dT_in, writes=["Dt"])
        V(lambda e: e.tensor_scalar(nsink_bc[:], sink_bc[:], -1.0, None, ALU.mult), r=["sink_bc"], w=["nsink_bc"])
        S.dma("sp", ks[:, 0:124, :], ck[:, 4:128, :], sem_key="o_kvs")
        S.dma("sp", vs[:, 0:124, :], cv[:, 4:128, :], sem_key="o_kvs")

        def ssm_setup():
            with ExitStack() as ses, ExitStack() as ses1:
                sbs = lambda n, s, d: ses.enter_context(nc.sbuf_tensor(n, s, d))
                sb1 = lambda n, s, d: ses1.enter_context(nc.sbuf_tensor(n, s, d))
                NK = 41
                X1 = sbs("X1", [128, 64, 16], F32); X2 = sbs("X2", [128, 64, 16], F32)
                Y1 = sbs("Y1", [128, 64, 16], F32); Y2 = sbs("Y2", [128, 64, 16], F32)
                PWr = sbs("PWr", [128, NK, 64], F32); PWi = sbs("PWi", [128, NK, 64], F32)
                bmk = sbs("bmk", [128, 128], F32)
                ddt = sbs("ddt", [128, 64], F32)
                lr = sb1("lr", [128, 64], F32); li = sb1("li", [128, 64], F32); dtt = sb1("dtt", [128, 64], F32)
                lrd = sb1("lrd", [128, 64], F32); lid = sb1("lid", [128, 64], F32)
                kv = sb1("kv", [128, NK], F32)
                bre = sb1("bre", [128, 64, 16], F32); bim = sb1("bim", [128, 64, 16], F32)
                cre = sb1("cre", [128, 64, 16], F32); cim = sb1("cim", [128, 64, 16], F32)
                bbr = sb1("bbr", [128, 64, 16], F32); bbi = sb1("bbi", [128, 64, 16], F32)
                ang = sb1("ang", [128, NK, 64], F32); mag = sb1("mag", [128, NK, 64], F32)
                tq = sb1("tq", [128, NK, 64], F32); tf = sb1("tf", [128, NK, 64], F32); mk = sb1("mk", [128, NK, 64], F32)
                ti = sb1("ti", [128, NK, 64], mybir.dt.int32)
                s1 = sb1("s1", [128, 64], F32); s2 = sb1("s2", [128, 64], F32); s3 = sb1("s3", [128, 64], F32)
                fre = sb1("fre", [128, 64], F32); fim = sb1("fim", [128, 64], F32)
                S.dma("sp", lr[:], lamT_re, writes=["lr"]); S.dma("sp", li[:], lamT_im, writes=["li"])
                S.dma("sp", dtt[:], ldt_in, writes=["dtt"]); S.dma("sp", kv[:], kvec_in, writes=["kv"])
                S.dma("sp", bre[:], bT_re, writes=["bre"]); S.dma("sp", bim[:], bT_im, writes=["bim"])
                S.dma("sp", cre[:], cT_re, writes=["cre"]); S.dma("sp", cim[:], cT_im, writes=["cim"])
                S.dma("sp", bmk[:], bmask_in, writes=["bmk"])
                S.dma("sp", ddt[:], dd_in, writes=["ddt"])
                A_(lambda e: e.activation(out=dtt[:], in_=dtt[:], func=AF.Exp), r=["dtt"], w=["dtt"])
                V(lambda e: e.tensor_tensor(out=lrd[:], in0=lr[:], in1=dtt[:], op=ALU.mult), r=["lr", "dtt"], w=["lrd"])
                V(lambda e: e.tensor_tensor(out=lid[:], in0=li[:], in1=dtt[:], op=ALU.mult), r=["li", "dtt"], w=["lid"])
                bck = lambda t: t[:].unsqueeze(1).to_broadcast([128, NK, 64])
                kvb = kv[:].unsqueeze(2).to_broadcast([128, NK, 64])
                V(lambda e: e.tensor_tensor(out=ang[:], in0=bck(lid), in1=kvb, op=ALU.mult), r=["lid", "kv"], w=["ang"])
                V(lambda e: e.tensor_tensor(out=mag[:], in0=bck(lrd), in1=kvb, op=ALU.mult), r=["lrd", "kv"], w=["mag"])
                A_(lambda e: e.activation(out=mag[:], in_=mag[:], func=AF.Exp), r=["mag"], w=["mag"])

                def sin_of(out, oname, phase):
                    V(lambda e: e.tensor_scalar(tq[:], ang[:], 1.0 / TWO_PI, phase, ALU.mult, ALU.add), r=["ang"], w=["tq"])
                    V(lambda e: e.tensor_copy(ti[:], tq[:]), r=["tq"], w=["ti"])
                    V(lambda e: e.tensor_copy(tf[:], ti[:]), r=["ti"], w=["tf"])
                    V(lambda e: e.tensor_tensor(out=tq[:], in0=tq[:], in1=tf[:], op=ALU.subtract), r=["tq", "tf"], w=["tq"])
                    V(lambda e: e.tensor_single_scalar(mk[:], tq[:], 0.5, ALU.is_gt), r=["tq"], w=["mk"])
                    V(lambda e: e.tensor_tensor(out=tq[:], in0=tq[:], in1=mk[:], op=ALU.subtract), r=["tq", "mk"], w=["tq"])
                    V(lambda e: e.tensor_single_scalar(mk[:], tq[:], -0.5, ALU.is_lt), r=["tq"], w=["mk"])
                    V(lambda e: e.tensor_tensor(out=tq[:], in0=tq[:], in1=mk[:], op=ALU.add), r=["tq", "mk"], w=["tq"])
                    A_(lambda e: e.activation(out=tf[:], in_=tq[:], func=AF.Sin, scale=TWO_PI), r=["tq"], w=["tf"])
                    V(lambda e: e.tensor_tensor(out=out[:], in0=tf[:], in1=mag[:], op=ALU.mult), r=["tf", "mag"], w=[oname])

                sin_of(PWi, "PWi", 0.0)
                sin_of(PWr, "PWr", 0.25)
                ar, ai = PWr[:, 32, :], PWi[:, 32, :]
                V(lambda e: e.tensor_scalar(s1[:], ar, -1.0, None, ALU.add), r=["PWr"], w=["s1"])
                V(lambda e: e.tensor_tensor(out=s2[:], in0=lr[:], in1=lr[:], op=ALU.mult), r=["lr"], w=["s2"])
                V(lambda e: e.tensor_tensor(out=s3[:], in0=li[:], in1=li[:], op=ALU.mult), r=["li"], w=["s3"])
                V(lambda e: e.tensor_tensor(out=s2[:], in0=s2[:], in1=s3[:], op=ALU.add), r=["s2", "s3"], w=["s2"])
                V(lambda e: e.reciprocal(s2[:], s2[:]), r=["s2"], w=["s2"])
                V(lambda e: e.tensor_tensor(out=fre[:], in0=s1[:], in1=lr[:], op=ALU.mult), r=["s1", "lr"], w=["fre"])
                V(lambda e: e.tensor_tensor(out=s3[:], in0=ai, in1=li[:], op=ALU.mult), r=["PWi", "li"], w=["s3"])
                V(lambda e: e.tensor_tensor(out=fre[:], in0=fre[:], in1=s3[:], op=ALU.add), r=["fre", "s3"], w=["fre"])
                V(lambda e: e.tensor_tensor(out=fre[:], in0=fre[:], in1=s2[:], op=ALU.mult), r=["fre", "s2"], w=["fre"])
                V(lambda e: e.tensor_tensor(out=fim[:], in0=ai, in1=lr[:], op=ALU.mult), r=["PWi", "lr"], w=["fim"])
                V(lambda e: e.tensor_tensor(out=s3[:], in0=s1[:], in1=li[:], op=ALU.mult), r=["s1", "li"], w=["s3"])
                V(lambda e: e.tensor_tensor(out=fim[:], in0=fim[:], in1=s3[:], op=ALU.subtract), r=["fim", "s3"], w=["fim"])
                V(lambda e: e.tensor_tensor(out=fim[:], in0=fim[:], in1=s2[:], op=ALU.mult), r=["fim", "s2"], w=["fim"])
                fb = lambda t: t[:].unsqueeze(2).to_broadcast([128, 64, 16])
                V(lambda e: e.tensor_tensor(out=bbr[:], in0=bre[:], in1=fb(fre), op=ALU.mult), r=["bre", "fre"], w=["bbr"])
                V(lambda e: e.tensor_tensor(out=X1[:], in0=bim[:], in1=fb(fim), op=ALU.mult), r=["bim", "fim"], w=["X1"])
                V(lambda e: e.tensor_tensor(out=bbr[:], in0=bbr[:], in1=X1[:], op=ALU.subtract), r=["bbr", "X1"], w=["bbr"])
                V(lambda e: e.tensor_tensor(out=bbi[:], in0=bim[:], in1=fb(fre), op=ALU.mult), r=["bim", "fre"], w=["bbi"])
                V(lambda e: e.tensor_tensor(out=X1[:], in0=bre[:], in1=fb(fim), op=ALU.mult), r=["bre", "fim"], w=["X1"])
                V(lambda e: e.tensor_tensor(out=bbi[:], in0=bbi[:], in1=X1[:], op=ALU.add), r=["bbi", "X1"], w=["bbi"])
                lo, hi = slice(0, 64), slice(64, 128)
                V(lambda e: e.tensor_copy(X1[lo], bbr[lo]), r=["bbr"], w=["X1"])
                V(lambda e: e.tensor_copy(X1[hi], bbi[hi]), r=["bbi"], w=["X1"])
                V(lambda e: e.tensor_scalar(X2[lo], bbi[lo], -1.0, None, ALU.mult), r=["bbi"], w=["X2"])
                V(lambda e: e.tensor_copy(X2[hi], bbr[hi]), r=["bbr"], w=["X2"])
                V(lambda e: e.tensor_copy(Y1[lo], cre[lo]), r=["cre"], w=["Y1"])
                V(lambda e: e.tensor_scalar(Y1[hi], cim[hi], -1.0, None, ALU.mult), r=["cim"], w=["Y1"])
                V(lambda e: e.tensor_scalar(Y2[lo], cim[lo], -1.0, None, ALU.mult), r=["cim"], w=["Y2"])
                V(lambda e: e.tensor_scalar(Y2[hi], cre[hi], -1.0, None, ALU.mult), r=["cre"], w=["Y2"])
                for (idx, R2, I2, rn, inn) in ((33, AR2, AI2, "AR2", "AI2"), (34, PR2, PI2, "PR2", "PI2")):
                    for s_ in range(2):
                        V(lambda e, s_=s_, R2=R2, idx=idx: e.tensor_copy(R2[:, s_, :], PWr[:, idx, :]), r=["PWr"], w=[rn])
                    V(lambda e, I2=I2, idx=idx: e.tensor_scalar(I2[lo, 0, :], PWi[lo, idx, :], -1.0, None, ALU.mult), r=["PWi"], w=[inn])
                    V(lambda e, I2=I2, idx=idx: e.tensor_copy(I2[hi, 0, :], PWi[hi, idx, :]), r=["PWi"], w=[inn])
                    V(lambda e, I2=I2, idx=idx: e.tensor_copy(I2[lo, 1, :], PWi[lo, idx, :]), r=["PWi"], w=[inn])
                    V(lambda e, I2=I2, idx=idx: e.tensor_scalar(I2[hi, 1, :], PWi[hi, idx, :], -1.0, None, ALU.mult), r=["PWi"], w=[inn])
                S.barrier()
                ses1.close()
                big1 = sbs("big1", [128, 64, 8, 16], F32); big2 = sbs("big2", [128, 64, 8, 16], F32)
                E = [sbs("E%d" % t, [128, 64, 128], BF16) for t in range(4)]
                stage = [sbs("stage0", [128, 64, 128], BF16)] * 2
                for lv, idx in enumerate((33, 35, 36, 37, 38, 39, 40)):
                    V(lambda e, lv=lv, idx=idx: e.tensor_copy(TR[:, lv, :], PWr[:, idx, :]), r=["PWr"], w=["TR"])
                    V(lambda e, lv=lv, idx=idx: e.tensor_scalar(TI[lo, lv, 0, :], PWi[lo, idx, :], -1.0, None, ALU.mult), r=["PWi"], w=["TI"])
                    V(lambda e, lv=lv, idx=idx: e.tensor_copy(TI[hi, lv, 0, :], PWi[hi, idx, :]), r=["PWi"], w=["TI"])
                    V(lambda e, lv=lv, idx=idx: e.tensor_copy(TI[lo, lv, 1, :], PWi[lo, idx, :]), r=["PWi"], w=["TI"])
                    V(lambda e, lv=lv, idx=idx: e.tensor_scalar(TI[hi, lv, 1, :], PWi[hi, idx, :], -1.0, None, ALU.mult), r=["PWi"], w=["TI"])
                for t, (Xa, Xb, xan, xbn) in enumerate(((X1, X2, "X1", "X2"), (X1, X2, "X1", "X2"),
                                                        (Y1, Y2, "Y1", "Y2"), (Y1, Y2, "Y1", "Y2"))):
                    prv = PWr[:, 8 * t:8 * t + 8, :].rearrange("p r g -> p g r").unsqueeze(3).to_broadcast([128, 64, 8, 16])
                    piv = PWi[:, 8 * t:8 * t + 8, :].rearrange("p r g -> p g r").unsqueeze(3).to_broadcast([128, 64, 8, 16])
                    xa = Xa[:].unsqueeze(2).to_broadcast([128, 64, 8, 16])
                    xb = Xb[:].unsqueeze(2).to_broadcast([128, 64, 8, 16])
                    V(lambda e, prv=prv, xa=xa: e.tensor_tensor(out=big1[:], in0=prv, in1=xa, op=ALU.mult), r=["PWr", xan], w=["big1"])
                    V(lambda e, piv=piv, xb=xb: e.tensor_tensor(out=big2[:], in0=piv, in1=xb, op=ALU.mult), r=["PWi", xbn], w=["big2"])
                    V(lambda e, t=t: e.tensor_tensor(out=E[t][:].rearrange("p g (r c) -> p g r c", r=8), in0=big1[:], in1=big2[:],
                                                     op=ALU.add), r=["big1", "big2"], w=["E%d" % t])
                S.dma("sp", Om_d, E[2][:], reads=["E2"], sem_key="tabw")
                for which, (dst, sti) in enumerate(((RT_d, 0), (RTs_d, 1), (Toep_d, 0))):
                    st, stn = stage[sti], "stage0"
                    for gb in range(16):
                        pm, pmn = next_ps()
                        for gg in range(4):
                            g = 4 * gb + gg
                            if which == 0:
                                lhs, rhs, rd = E[0][:, g, :], ident[:], ["E0", "ident"]
                            elif which == 1:
                                lhs, rhs, rd = E[0][:, g, :], psw[:], ["E0", "psw"]
                            else:
                                lhs, rhs, rd = E[1][:, g, :], E[3][:, g, :], ["E1", "E3"]
                            S.op("pe", lambda e, pm=pm, gg=gg, lhs=lhs, rhs=rhs: e.matmul(
                                pm[:, gg * 128:(gg + 1) * 128], lhs, rhs, start=True, stop=True), reads=rd, writes=[pmn])
                        if which == 2:
                            V(lambda e, pm=pm: e.tensor_tensor(
                                out=big1[:, 0:4, :, :].rearrange("p g r c -> p g (r c)"), in0=pm[:, :].rearrange("p (g m) -> p g m", g=4),
                                in1=bmk[:].unsqueeze(1).to_broadcast([128, 4, 128]), op=ALU.mult), r=[pmn, "bmk"], w=["big1"])
                            for gg in range(4):
                                g = 4 * gb + gg
                                V(lambda e, gg=gg, g=g, st=st: e.scalar_tensor_tensor(
                                    out=st[:, g, :], in0=identf[:], scalar=ddt[:, g:g + 1],
                                    in1=big1[:, gg, :, :].rearrange("p r c -> p (r c)"), op0=ALU.mult, op1=ALU.add),
                                    r=["identf", "ddt", "big1"], w=[stn])
                        else:
                            A_(lambda e, pm=pm, gb=gb, st=st: e.copy(out=st[:, 4 * gb:4 * gb + 4, :],
                                                                     in_=pm[:, :].rearrange("p (g m) -> p g m", g=4)), r=[pmn], w=[stn])
                    S.dma("sp", dst, st[:], reads=[stn], sem_key="tabw")
                S.barrier()

        if WCONV:
            for wid in (0, 1, 2, 3, 4, 5):
                sv = SRCV[wid]
                dv, dn = BFV[wid]
                ncols_ = sv.shape[2]
                for c0 in range(0, ncols_, 512):
                    S.dma("pool", dv[:, :, c0:c0 + 512], sv[:, :, c0:c0 + 512], writes=[dn], sem_key=dn)
        if ENABLE_SSM:
            ssm_setup()
        selR = sb("selR", [128, 64, 128], BF16)
        x_tok = sb("x_tok", [128, 4, D], F32)
        h_tok = sb("h_tok", [128, 4, D], BF16)
        ssq = sb("ssq", [128, 8], F32)
        rstd = sb("rstd", [128, 8], F32)
        hT = sb("hT", [128, 8, 512], BF16)
        kT2 = sb("kT2", [128, 4, 640], BF16)
        vpad = sb("vpad", [128, 5, 4, 2, 128], BF16)
        kv_tok = sb("kv_tok", [128, 4, 512], F32)
        gyT = sb("gyT", [128, 8, 512], BF16)
        sm = sb("sm", [128, 32], F32)
        msc = sb("msc", [16, 128], F32); msn = sb("msn", [16, 4], F32)
        sink_s = sb("sink_s", [16, 4], F32); nsink_s = sb("nsink_s", [16, 4], F32)
        vnf = sb("vnf", [4, 256], F32); vnb = sb("vnb", [4, 256], BF16)
        vnf2 = sb("vnf2", [4, 256], F32); vnb2 = sb("vnb2", [4, 256], BF16)
        wslot = [sb("wslot%d" % i, [128, 8192], BF16) for i in range(2)]
        V(lambda e: e.memset(vpad[:], 0.0), w=["vpad"])
        S.dma("sp", msc[:], msc_in, writes=["msc"]); S.dma("sp", msn[:], msn_in, writes=["msn"])
        S.dma("sp", sink_s[:], sink_s_in, writes=["sink_s"])
        V(lambda e: e.tensor_scalar(nsink_s[:], sink_s[:], -1.0, None, ALU.mult), r=["sink_s"], w=["nsink_s"])
        for hh_ in range(2):
            S.dma("pool", selR[:, 32 * hh_:32 * hh_ + 32, :], selc[:, 32 * hh_:32 * hh_ + 32, :], writes=["selR"], sem_key="selR")

        work = sb("work", [128, 32768], BF16)
        KB = 512
        uT = work[:, 0:8 * KB].rearrange("p (k m) -> p k m", k=8)
        Up = work[:, 8 * KB:16 * KB].rearrange("p (g n) -> p g n", g=64)
        SSx = work[:, 16 * KB:48 * KB].bitcast(F32).rearrange("p (n s g) -> p n s g", n=64, s=2)
        Hp = work[:, 48 * KB:56 * KB].rearrange("p (g n) -> p g n", g=64)
        Yp = work[:, 56 * KB:64 * KB].rearrange("p (g n) -> p g n", g=64)
        qT = work[:, 0:8 * KB].rearrange("p (k m) -> p k m", k=8)
        oT = work[:, 8 * KB:16 * KB].rearrange("p (k m) -> p k m", k=8)
        mA = work[:, 16 * KB:24 * KB].rearrange("p (k m) -> p k m", k=8)
        mB = work[:, 24 * KB:32 * KB].rearrange("p (k m) -> p k m", k=8)
        aT = work[:, 32 * KB:48 * KB].rearrange("p (k m) -> p k m", k=16)
        rtmp = work[:, 56 * KB:58 * KB].bitcast(F32)
        sc_ = [work[:, 48 * KB:52 * KB].bitcast(F32).rearrange("p (h s) -> p h s", h=4),
               work[:, 58 * KB:62 * KB].bitcast(F32).rearrange("p (h s) -> p h s", h=4)]
        Pb_ = [work[:, 52 * KB:54 * KB].rearrange("p (h s) -> p h s", h=4),
               work[:, 62 * KB:64 * KB].rearrange("p (h s) -> p h s", h=4)]
        PT_ = [work[:, 54 * KB:56 * KB].rearrange("p (h s) -> p h s", h=8),
               work[:, 56 * KB:58 * KB].rearrange("p (h s) -> p h s", h=8)]
        sm_ = [sm, sb("sm2", [128, 32], F32)]
        ytmp = sb("ytmp", [128, 512], F32)
        ztmp = sb("ztmp", [128, 512], F32)
        junk = ztmp[:].bitcast(BF16)
        stmp = sb("stmp", [128, 128], F32)
        WORK_NAMES = ["HpYp", "uT", "Up", "SSx", "Hp", "Yp", "qT", "oT", "mA", "mB", "aT", "sc0", "sc1", "Pb0", "Pb1", "PT0", "PT1", "rtmp"]

        def phase_barrier():
            S.barrier()
            for n in WORK_NAMES:
                S.wr.pop(n, None)
                S.rd.pop(n, None)


        def piped(tag):
            def deco(fn):
                def wrapped(*a, **k):
                    old = S.pipe
                    S.pipe = tag in PIPE
                    try:
                        return fn(*a, **k)
                    finally:
                        S.pipe = old
                return wrapped
            return deco

        def rms_stats(xt, xname, nsub, psz):
            for j in range(nsub):
                A_(lambda e, j=j: e.activation(out=junk[:psz, :], in_=xt[:psz, j, :], func=AF.Square,
                                               accum_out=ssq[:psz, j:j + 1]), r=[xname], w=["ztmp", "ssq"])
            V(lambda e: e.tensor_scalar(rstd[:psz, :nsub], ssq[:psz, :nsub], 1.0 / D, EPS, ALU.mult, ALU.add), r=["ssq"], w=["rstd"])
            A_(lambda e: e.activation(out=rstd[:psz, :nsub], in_=rstd[:psz, :nsub], func=AF.Sqrt), r=["rstd"], w=["rstd"])
            V(lambda e: e.reciprocal(rstd[:psz, :nsub], rstd[:psz, :nsub]), r=["rstd"], w=["rstd"])

        @piped("H")
        def rmsnorm_to_T(xt, xname, gt, gname, nsub, psz):
            rms_stats(xt, xname, nsub, psz)
            for j in range(nsub):
                V(lambda e, j=j: e.scalar_tensor_tensor(out=h_tok[:psz, j, :], in0=xt[:psz, j, :], scalar=rstd[:psz, j:j + 1],
                                                        in1=gt[:psz, :], op0=ALU.mult, op1=ALU.mult),
                  r=[xname, "rstd", gname], w=["h_tok"])
            for j in range(nsub):
                pt, pn = next_pt()
                for k in range(8):
                    S.op("pe", lambda e, k=k, j=j, pt=pt: e.transpose(
                        pt[:, k, :psz], h_tok[:psz, j, k * 128:(k + 1) * 128], ident[:psz, :psz]),
                        reads=["h_tok", "ident"], writes=[pn], inc=(k == 7), mode=("T" if psz == 128 else "T64"))
                A_(lambda e, j=j, pt=pt: e.copy(out=hT[:, :, j * 128:j * 128 + psz], in_=pt[:, :, :psz]), r=[pn], w=["hT"])

        def fm_proj(wv, wn, m, src, sname, nt, nk=8):
            pm, pmn = next_ps()
            S.pipe = "A" in PIPE
            for k in range(nk):
                S.op("pe", lambda e, k=k: e.matmul(pm[:, :nt], wv[:, k, m * 128:(m + 1) * 128], src[:, k, :nt],
                                                   start=(k == 0), stop=(k == nk - 1)),
                     reads=[sname, wn], writes=[pmn], inc=(k == nk - 1))
            S.pipe = False
            return pm, pmn

        @piped("G")
        def kv_token_major(wkv, wkv_n, j, psz, blk):
            pm, pmn = next_ps()
            for k in range(8):
                S.op("pe", lambda e, k=k: e.matmul(pm[:psz, :], hT[:, k, j * 128:j * 128 + psz], wkv[:, k, 0:512],
                                                   start=(k == 0), stop=(k == 7)),
                     reads=["hT", wkv_n], writes=[pmn], inc=(k == 7), mode=("full" if psz == 128 else "m64"))
            V(lambda e: e.tensor_copy(kv_tok[:psz, j, :], pm[:psz, :]), r=[pmn], w=["kv_tok"])
            if blk is not None:
                vsrc = kv_tok[:psz, j, 256:512].rearrange("p (h d) -> p h d", h=4)
                A_(lambda e: e.copy(out=vpad[:psz, blk, :, 0, 0:64], in_=vsrc), r=["kv_tok"], w=["vpad"])
                A_(lambda e: e.copy(out=vpad[:psz, blk, :, 1, 64:128], in_=vsrc), r=["kv_tok"], w=["vpad"])

        @piped("G")
        def k_feature_major(wkd, wkd_n, nt, col0):
            for h in range(4):
                pm, pmn = fm_proj(wkd, wkd_n, h, hT, "hT", nt)
                A_(lambda e, h=h, pm=pm: e.copy(out=kT2[:, h, col0:col0 + nt], in_=pm[:, :nt]), r=[pmn], w=["kT2"])

        @piped("F")
        def attention_s1(j, kvh, mask_t, mask_n, par):
            sc, Pb, PT, sm = sc_[par], Pb_[par], PT_[par], sm_[par]
            scn, Pbn, PTn, smn = "sc%d" % par, "Pb%d" % par, "PT%d" % par, "sm%d" % par
            ptw = [PTn] + (["rtmp"] if par == 1 else [])
            pss = []
            for i2 in range(2):
                pm, pmn = next_ps()
                pss.append((pm, pmn))
                for a in range(2):
                    tile, base = 2 * kvh + a, 64 * i2
                    S.op("pe", lambda e, pm=pm, a=a, tile=tile, base=base: e.matmul(
                        pm[:, a * 256:(a + 1) * 256], qT[base:base + 64, tile, j * 128:(j + 1) * 128],
                        kT2[base:base + 64, kvh, j * 128:(j + 2) * 128], start=True, stop=True),
                        reads=["qT", "kT2"], writes=[pmn], inc=(a == 1), mode="k64")
            for i2 in range(2):
                pm, pmn = pss[i2]
                V(lambda e, pm=pm, i2=i2: e.tensor_tensor(
                    out=sc[:, i2:4:2, :], in0=pm[:, :].rearrange("p (h s) -> p h s", h=2),
                    in1=mask_t[:].unsqueeze(1).to_broadcast([128, 2, 256]), op=ALU.add), r=[pmn, mask_n], w=[scn])
            negm, rs, es_, den = sm[:, 0:4], sm[:, 4:8], sm[:, 8:12], sm[:, 12:16]
            V(lambda e: e.reduce_max(out=negm, in_=sc[:], axis=AX.X, negate=True), r=[scn], w=[smn])
            V(lambda e: e.tensor_tensor(out=negm, in0=negm, in1=nsink_bc[:, 4 * kvh:4 * kvh + 4], op=ALU.min),
              r=[smn, "nsink_bc"], w=[smn])
            for i in range(4):
                A_(lambda e, i=i: e.activation(out=Pb[:, i, :], in_=sc[:, i, :], func=AF.Exp,
                                               bias=negm[:, i:i + 1], accum_out=rs[:, i:i + 1]), r=[scn, smn], w=[Pbn, smn])
            V(lambda e: e.tensor_tensor(out=es_, in0=negm, in1=sink_bc[:, 4 * kvh:4 * kvh + 4], op=ALU.add),
              r=[smn, "sink_bc"], w=[smn])
            A_(lambda e: e.activation(out=es_, in_=es_, func=AF.Exp), r=[smn], w=[smn])
            V(lambda e: e.tensor_tensor(out=den, in0=rs, in1=es_, op=ALU.add), r=[smn], w=[smn])
            V(lambda e: e.reciprocal(den, den), r=[smn], w=[smn])
            V(lambda e: e.tensor_tensor(out=Pb[:], in0=Pb[:], in1=den.unsqueeze(2).to_broadcast([128, 4, 256]), op=ALU.mult),
              r=[Pbn, smn], w=[Pbn])

        @piped("F")
        def attention_s2(j, kvh, par):
            sc, Pb, PT, sm = sc_[par], Pb_[par], PT_[par], sm_[par]
            scn, Pbn, PTn, smn = "sc%d" % par, "Pb%d" % par, "PT%d" % par, "sm%d" % par
            ptw = [PTn] + (["rtmp"] if par == 1 else [])
            pt, pn = next_pt()
            for i in range(4):
                for blk in range(2):
                    S.op("pe", lambda e, i=i, blk=blk: e.transpose(pt[:, 2 * i + blk, :], Pb[:, i, blk * 128:(blk + 1) * 128], ident[:]),
                         reads=[Pbn, "ident"], writes=[pn], inc=(i == 3 and blk == 1), mode="T")
            A_(lambda e: e.copy(out=PT[:], in_=pt[:]), r=[pn], w=ptw)
            for a in range(2):
                pm, pmn = next_ps()
                n = 0
                for i2 in range(2):
                    for blk in range(2):
                        S.op("pe", lambda e, pm=pm, i2=i2, blk=blk, n=n: e.matmul(
                            pm[:, 0:128], vpad[:, j + blk, kvh, i2, :], PT[:, 2 * (2 * a + i2) + blk, :],
                            start=(n == 0), stop=(n == 3)), reads=["vpad"] + ptw, writes=[pmn], inc=(n == 3))
                        n += 1
                V(lambda e, pm=pm, a=a: e.tensor_copy(oT[:, 2 * kvh + a, j * 128:(j + 1) * 128], pm[:, 0:128]), r=[pmn], w=["oT"])

        def ssm_u(nt, bf=True):
            wu, wu_n = load_w(w_in_v, 8, 1536, 1024, bf=bf)
            for m in range(8):
                pm, pmn = fm_proj(wu, wu_n, m, hT, "hT", nt)
                A_(lambda e, m=m, pm=pm: e.copy(out=uT[:, m, :nt], in_=pm[:, :nt]), r=[pmn], w=["uT"])

        @piped("B")
        def ssm_relayout(ncols, r_list, colsl):
            for q in range(8):
                pm, pmn = next_ps()
                for gl_ in range(8):
                    for ri, r in enumerate(r_list):
                        S.op("pe", lambda e, pm=pm, gl_=gl_, r=r, q=q: e.matmul(
                            pm[:, gl_ * 64:gl_ * 64 + ncols], selR[:, gl_ * 8 + r, :], uT[:, q, colsl(r)],
                            start=(ri == 0), stop=(ri == len(r_list) - 1)),
                            reads=["selR", "uT"], writes=[pmn], inc=(gl_ == 7 and ri == len(r_list) - 1))
                A_(lambda e, pm=pm, q=q: e.copy(out=Up[:, 8 * q:8 * q + 8, :ncols],
                                               in_=pm[:, :].rearrange("p (g n) -> p g n", g=8)[:, :, :ncols]), r=[pmn], w=["Up"])


        @piped("B")
        def ssm_relayout4():
            Upv = Up[:, :, :].rearrange("p (q i l) n -> p q i l n", q=8, i=4)
            for qh in range(2):
                banks = [next_ps() for _ in range(4)]
                for ql in range(4):
                    q = 4 * qh + ql
                    for l in range(2):
                        for r in range(8):
                            for i in range(4):
                                gl_ = 2 * i + l
                                pm, pmn = banks[i]
                                col = (ql * 2 + l) * 64
                                last = (ql == 3 and l == 1 and r == 7)
                                S.op("pe", lambda e, pm=pm, i=i, gl_=gl_, r=r, q=q, col=col: e.matmul(
                                    pm[:, col:col + 64], selR[32 * i:32 * i + 32, gl_ * 8 + r, :],
                                    uT[32 * i:32 * i + 32, q, r:512:8], start=(r == 0), stop=(r == 7),
                                    tile_position=(32 * i, 0)),
                                    reads=["selR", "uT"], writes=[pmn], inc=last, mode="k32")
                for i in range(4):
                    pm, pmn = banks[i]
                    A_(lambda e, pm=pm, i=i, qh=qh: e.copy(
                        out=Upv[:, 4 * qh:4 * qh + 4, i, :, :],
                        in_=pm[:, :].rearrange("p (q l n) -> p q l n", q=4, l=2)), r=[pmn], w=["Up"])

        @piped("C")
        def ssm_S(ncols):
            for s_, src in enumerate((RT_d, RTs_d)):
                tab, tabn = load_tab(src)
                gpb = 512 // max(ncols, 1) if ncols >= 64 else 8
                gpb = 8
                for gb in range(64 // gpb):
                    pm, pmn = next_ps()
                    for gg in range(gpb):
                        g = gb * gpb + gg
                        S.op("pe", lambda e, pm=pm, gg=gg, g=g, tab=tab: e.matmul(
                            pm[:, gg * 64:gg * 64 + ncols], tab[:, g, :], Up[:, g, :ncols], start=True, stop=True),
                            reads=[tabn, "Up"], writes=[pmn], inc=(gg == gpb - 1))
                    A_(lambda e, pm=pm, gb=gb, s_=s_: e.copy(
                        out=SSx[:, 0:ncols, s_, gb * gpb:(gb + 1) * gpb],
                        in_=pm[:, :].rearrange("p (g n) -> p n g", g=8)[:, :ncols, :]), r=[pmn], w=["SSx"])

        def ssm_recur(ncols, store_hp):
            for n in range(ncols):
                if store_hp:
                    V(lambda e, n=n: e.tensor_copy(Hp[:, :, n], X[:, 0, :]), r=["X"], w=["Hp"])
                V(lambda e: e.tensor_tensor(out=t1[:], in0=AR2[:], in1=X[:, 0:2, :], op=ALU.mult), r=["AR2", "X"], w=["t1"])
                V(lambda e: e.tensor_tensor(out=t2[:], in0=AI2[:], in1=X[:, 1::-1, :], op=ALU.mult), r=["AI2", "X"], w=["t2"])
                V(lambda e: e.tensor_tensor(out=t1[:], in0=t1[:], in1=t2[:], op=ALU.add), r=["t1", "t2"], w=["t1"])
                V(lambda e, n=n: e.tensor_tensor(out=X[:, 0:2, :], in0=t1[:], in1=SSx[:, n, :, :], op=ALU.add),
                  r=["t1", "SSx"], w=["X"])


        def ssm_reduce(part):
            bufA = work[:, 16 * KB:48 * KB].bitcast(F32)
            bufB = work[:, 48 * KB:64 * KB].bitcast(F32)
            bufC = gyT[:].rearrange("p a b -> p (a b)").bitcast(F32)
            tmp = kv_tok[:].rearrange("p a b -> p (a b)")
            chain = [(bufA, "SSx"), (bufB, "HpYp"), (bufC, "gyT"), (bufB, "HpYp"), (bufC, "gyT"), (bufB, "HpYp"), (bufC, "gyT")]
            for lv in (range(0, 1) if part == 0 else range(1, 6)):
                (src, srcn), (dst, dstn), n = chain[lv], chain[lv + 1], 64 >> lv
                m = n // 2
                sv = src[:, 0:n * 128].rearrange("p (m e s g) -> p m e s g", e=2, s=2, g=64)
                ev, od = sv[:, :, 0], sv[:, :, 1]
                dv = dst[:, 0:m * 128].rearrange("p (m s g) -> p m s g", s=2, g=64)
                tv = tmp[:, 0:m * 64].rearrange("p (m g) -> p m g", g=64)
                V(lambda e, dv=dv, ev=ev, lv=lv, m=m: e.tensor_tensor(
                    out=dv, in0=ev, in1=TR[:, lv, :].unsqueeze(1).unsqueeze(1).to_broadcast([128, m, 2, 64]), op=ALU.mult), r=[srcn, "TR"], w=[dstn])
                for s_ in range(2):
                    V(lambda e, tv=tv, ev=ev, lv=lv, m=m, s_=s_: e.tensor_tensor(
                        out=tv, in0=ev[:, :, 1 - s_, :], in1=TI[:, lv, s_, :].unsqueeze(1).to_broadcast([128, m, 64]), op=ALU.mult),
                        r=[srcn, "TI"], w=["kv_tok"])
                    V(lambda e, tv=tv, dv=dv, s_=s_: e.tensor_tensor(out=dv[:, :, s_, :], in0=dv[:, :, s_, :], in1=tv, op=ALU.add),
                      r=[dstn, "kv_tok"], w=[dstn])
                V(lambda e, dv=dv, od=od: e.tensor_tensor(out=dv, in0=dv, in1=od, op=ALU.add), r=[dstn, srcn], w=[dstn])
            if part == 0:
                return
            src, srcn = chain[6]
            fin = src[:, 0:128].rearrange("p (s g) -> p s g", s=2)
            V(lambda e: e.tensor_tensor(out=t1[:], in0=TR[:, 6, :].unsqueeze(1).to_broadcast([128, 2, 64]), in1=X[:, 0:2, :], op=ALU.mult), r=["TR", "X"], w=["t1"])
            V(lambda e: e.tensor_tensor(out=t2[:], in0=TI[:, 6], in1=X[:, 1::-1, :], op=ALU.mult), r=["TI", "X"], w=["t2"])
            V(lambda e: e.tensor_tensor(out=t1[:], in0=t1[:], in1=t2[:], op=ALU.add), r=["t1", "t2"], w=["t1"])
            V(lambda e, fin=fin: e.tensor_tensor(out=X[:, 0:2, :], in0=t1[:], in1=fin, op=ALU.add), r=["t1", srcn], w=["X"])

        @piped("D")
        def ssm_Y(ncols):
            tT, tTn = load_tab(Toep_d)
            tO, tOn = load_tab(Om_d)
            for gb in range(8):
                pm, pmn = next_ps()
                for gg in range(8):
                    g = gb * 8 + gg
                    S.op("pe", lambda e, pm=pm, gg=gg, g=g: e.matmul(
                        pm[:, gg * 64:gg * 64 + ncols], tT[:, g, :], Up[:, g, :ncols], start=True, stop=False),
                        reads=[tTn, "Up"], writes=[pmn], inc=False)
                    S.op("pe", lambda e, pm=pm, gg=gg, g=g: e.matmul(
                        pm[:, gg * 64:gg * 64 + ncols], tO[:, g, :], Hp[:, g, :ncols], start=False, stop=True),
                        reads=[tOn, "Hp"], writes=[pmn], inc=(gg == 7))
                A_(lambda e, pm=pm, gb=gb: e.copy(out=Yp[:, 8 * gb:8 * gb + 8, :ncols],
                                                 in_=pm[:, :].rearrange("p (g n) -> p g n", g=8)[:, :, :ncols]), r=[pmn], w=["Yp"])

        @piped("E")
        def ssm_back(nt, ncols, r_list):
            tS, tSn = load_tab(selTc)
            nr = len(r_list)
            for q in range(8):
                pm, pmn = next_ps()
                for ri, r in enumerate(r_list):
                    for gl_ in range(8):
                        S.op("pe", lambda e, pm=pm, ri=ri, r=r, gl_=gl_, q=q: e.matmul(
                            pm[:, ri * ncols:(ri + 1) * ncols], tS[:, gl_ * 8 + r, :], Yp[:, 8 * q + gl_, :ncols],
                            start=(gl_ == 0), stop=(gl_ == 7)), reads=[tSn, "Yp"], writes=[pmn],
                            inc=(gl_ == 7 and ri == nr - 1))
                pv = pm[:, 0:nr * ncols].rearrange("p (r n) -> p n r", r=nr)
                uv = uT[:, q, :nt].rearrange("p (n r) -> p n r", r=nr)
                yv = ytmp[:, :nt].rearrange("p (n r) -> p n r", r=nr)
                V(lambda e, pv=pv, yv=yv: e.tensor_copy(yv, pv), r=[pmn], w=["ytmp"])
                V(lambda e: e.tensor_tensor(out=ztmp[:, :nt], in0=ytmp[:, :nt], in1=ytmp[:, :nt], op=ALU.mult), r=["ytmp"], w=["ztmp"])
                V(lambda e: e.tensor_scalar(ztmp[:, :nt], ztmp[:, :nt], 0.044715, 1.0, ALU.mult, ALU.add), r=["ztmp"], w=["ztmp"])
                V(lambda e: e.tensor_tensor(out=ztmp[:, :nt], in0=ztmp[:, :nt], in1=ytmp[:, :nt], op=ALU.mult), r=["ztmp", "ytmp"], w=["ztmp"])
                A_(lambda e: e.activation(out=ztmp[:, :nt], in_=ztmp[:, :nt], func=AF.Sigmoid, scale=1.5957691216057308),
                   r=["ztmp"], w=["ztmp"])
                V(lambda e, q=q: e.tensor_tensor(out=gyT[:, q, :nt], in0=ztmp[:, :nt], in1=ytmp[:, :nt], op=ALU.mult),
                  r=["ztmp", "ytmp"], w=["gyT"])

        def state_out(src_ap, src_names, ncols, dst):
            for c0 in range(0, ncols, 128):
                cw = min(128, ncols - c0)
                pm, pmn = next_ps()
                S.op("pe", lambda e, pm=pm, c0=c0, cw=cw: e.transpose(pm[:cw, 0:128], src_ap[:, c0:c0 + cw], identf[:]),
                     reads=list(src_names) + ["identf"], writes=[pmn], mode="Tf32")
                V(lambda e, pm=pm, cw=cw: e.tensor_copy(stmp[:cw, 0:128], pm[:cw, 0:128]), r=[pmn], w=["stmp"])
                S.dma("sp", dst[c0:c0 + cw, :], stmp[:cw, 0:128], reads=["stmp"], sem_key="o_st")


        @piped("I")
        def sample_attention():
            SA = work[:, 32 * KB:48 * KB]
            qs = SA[0:64, 0:1024].rearrange("p (b k i t) -> p b k i t", b=16, k=4, i=4)
            qsv = SA[0:64, 0:1024].rearrange("p (b k i t) -> p (k i) b t", b=16, k=4, i=4)
            ksT = SA[0:64, 1024:1280].rearrange("p (h t) -> p h t", h=4)
            SAB = []
            for par in range(2):
                o = 1280 + par * 2048
                SAB.append((SA[0:16, o:o + 1024].bitcast(F32).rearrange("p (h s) -> p h s", h=4),
                            SA[0:16, o + 1024:o + 1056].bitcast(F32).rearrange("p (h s) -> p h s", h=4),
                            SA[0:16, o + 1056:o + 1568].rearrange("p (h s) -> p h s", h=4),
                            SA[0:16, o + 1568:o + 1584].rearrange("p (h s) -> p h s", h=4),
                            SA[:, o + 1584:o + 1648].rearrange("p (h s) -> p h s", h=4),
                            SA[0:4, o + 1648:o + 1712].rearrange("p (h s) -> p h s", h=4),
                            SA[0:16, o + 1712:o + 1968],
                            SA[0:16, o + 1968:o + 2032].bitcast(F32)))
            vn_ = [(vnf, vnb), (vnf2, vnb2)]
            wq, wq_n = load_w(w_in_v, 8, 0, 1024)
            for hb in range(2):
                pm, pmn = next_ps()
                for hh_ in range(8):
                    h = hb * 8 + hh_
                    for k in range(8):
                        S.op("pe", lambda e, pm=pm, hh_=hh_, h=h, k=k: e.matmul(
                            pm[:64, hh_ * 64:(hh_ + 1) * 64], wq[:, k, h * 64:(h + 1) * 64], hT[:, k, :64],
                            start=(k == 0), stop=(k == 7)), reads=[wq_n, "hT"], writes=[pmn], inc=(hh_ == 7 and k == 7), mode="m64")
                A_(lambda e, pm=pm, hb=hb: e.activation(out=qsv[:, hb * 8:(hb + 1) * 8, :, :],
                                                       in_=pm[:64, :].rearrange("p (h b t) -> p h b t", h=8, b=16), func=AF.Copy, scale=0.125),
                   r=[pmn], w=["qs"])
            wk, wk_n = load_w(w_in_v, 8, 1024, 256)
            pm, pmn = next_ps()
            for kvh in range(4):
                for k in range(8):
                    S.op("pe", lambda e, pm=pm, kvh=kvh, k=k: e.matmul(
                        pm[:64, kvh * 64:(kvh + 1) * 64], wk[:, k, kvh * 64:(kvh + 1) * 64], hT[:, k, :64],
                        start=(k == 0), stop=(k == 7)), reads=[wk_n, "hT"], writes=[pmn], inc=(kvh == 3 and k == 7), mode="m64")
            A_(lambda e, pm=pm: e.copy(out=ksT[:], in_=pm[:64, 0:256].rearrange("p (h t) -> p h t", h=4)), r=[pmn], w=["ksT"])
            slot, kvn = next_slot()
            kcv = slot[:, :].rearrange("p (w b c) -> p w b c", w=2, b=16)
            for w_, srcc in enumerate((ck, cv)):
                for hb in range(2):
                    S.dma("pool", kcv[:, w_, 8 * hb:8 * hb + 8, :], srcc[8 * hb:8 * hb + 8].rearrange("b s c -> s b c"), writes=[kvn])
            slot2, kTn = next_slot()
            kcT = slot2[0:64, :].rearrange("p (u s) -> p u s", u=64)
            for b in range(16):
                pt, pn = next_pt()
                for kvh in range(4):
                    S.op("pe", lambda e, pt=pt, kvh=kvh, b=b: e.transpose(pt[:64, kvh, :], kcv[:, 0, b, kvh * 64:(kvh + 1) * 64], ident[:]),
                         reads=[kvn, "ident"], writes=[pn], inc=(kvh == 3), mode="Tm64")
                A_(lambda e, pt=pt, b=b: e.copy(out=kcT[:, 4 * b:4 * b + 4, :], in_=pt[:64, 0:4, :]), r=[pn], w=[kTn])
            def sa_s1(b):
                    par = b % 2
                    scc, scn, Pc, Pn, PcT, PnT, ob, st_ = SAB[par]
                    vnfl, vnbl = vn_[par]
                    negm, m2, rs1, rs2, es_, den = (st_[:, 0:4], st_[:, 4:8], st_[:, 8:12], st_[:, 12:16], st_[:, 16:20], st_[:, 20:24])
                    N_ = lambda x: "%s_%d" % (x, par)
                    S.dma("sp", vnfl[:, :], kv_tok[4 * b:4 * b + 4, 0, 256:512], reads=["kv_tok"], writes=[N_("vnf")])
                    A_(lambda e: e.copy(out=vnbl[:, :], in_=vnfl[:, :]), r=[N_("vnf")], w=[N_("vnb")])
                    pmc, pcn = next_ps()
                    pmn_, pnn = next_ps()
                    for kvh in range(4):
                        lhs = qs[:, b, kvh, :, :].rearrange("p i t -> p (i t)")
                        S.op("pe", lambda e, pmc=pmc, kvh=kvh, lhs=lhs, b=b: e.matmul(
                            pmc[:16, kvh * 128:(kvh + 1) * 128], lhs, kcT[:, 4 * b + kvh, :], start=True, stop=True),
                            reads=["qs", kTn], writes=[pcn], inc=(kvh == 3), mode="k64m32")
                    for kvh in range(4):
                        lhs = qs[:, b, kvh, :, :].rearrange("p i t -> p (i t)")
                        S.op("pe", lambda e, pmn_=pmn_, kvh=kvh, lhs=lhs, b=b: e.matmul(
                            pmn_[:16, kvh * 4:(kvh + 1) * 4], lhs, ksT[:, kvh, 4 * b:4 * b + 4], start=True, stop=True),
                            reads=["qs", "ksT"], writes=[pnn], inc=(kvh == 3), mode="k64m32")
                    V(lambda e, pmc=pmc: e.tensor_tensor(out=scc, in0=pmc[:16, :].rearrange("p (h s) -> p h s", h=4),
                                                         in1=msc[:].unsqueeze(1).to_broadcast([16, 4, 128]), op=ALU.add), r=[pcn, "msc"], w=[N_("scc")])
                    V(lambda e, pmn_=pmn_: e.tensor_tensor(out=scn, in0=pmn_[:16, 0:16].rearrange("p (h s) -> p h s", h=4),
                                                           in1=msn[:].unsqueeze(1).to_broadcast([16, 4, 4]), op=ALU.add), r=[pnn, "msn"], w=[N_("scn")])
                    V(lambda e: e.reduce_max(out=negm, in_=scc, axis=AX.X, negate=True), r=[N_("scc")], w=[N_("st_")])
                    V(lambda e: e.reduce_max(out=m2, in_=scn, axis=AX.X, negate=True), r=[N_("scn")], w=[N_("st_")])
                    V(lambda e: e.tensor_tensor(out=negm, in0=negm, in1=m2, op=ALU.min), r=[N_("st_")], w=[N_("st_")])
                    V(lambda e: e.tensor_tensor(out=negm, in0=negm, in1=nsink_s[:], op=ALU.min), r=[N_("st_"), "nsink_s"], w=[N_("st_")])
                    for kvh in range(4):
                        A_(lambda e, kvh=kvh: e.activation(out=Pc[:, kvh, :], in_=scc[:, kvh, :], func=AF.Exp, bias=negm[:, kvh:kvh + 1],
                                                           accum_out=rs1[:, kvh:kvh + 1]), r=[N_("scc"), N_("st_")], w=[N_("Pc"), N_("st_")])
                        A_(lambda e, kvh=kvh: e.activation(out=Pn[:, kvh, :], in_=scn[:, kvh, :], func=AF.Exp, bias=negm[:, kvh:kvh + 1],
                                                           accum_out=rs2[:, kvh:kvh + 1]), r=[N_("scn"), N_("st_")], w=[N_("Pn"), N_("st_")])
                    V(lambda e: e.tensor_tensor(out=es_, in0=negm, in1=sink_s[:], op=ALU.add), r=[N_("st_"), "sink_s"], w=[N_("st_")])
                    A_(lambda e: e.activation(out=es_, in_=es_, func=AF.Exp), r=[N_("st_")], w=[N_("st_")])
                    V(lambda e: e.tensor_tensor(out=den, in0=rs1, in1=rs2, op=ALU.add), r=[N_("st_")], w=[N_("st_")])
                    V(lambda e: e.tensor_tensor(out=den, in0=den, in1=es_, op=ALU.add), r=[N_("st_")], w=[N_("st_")])
                    V(lambda e: e.reciprocal(den, den), r=[N_("st_")], w=[N_("st_")])
                    V(lambda e: e.tensor_tensor(out=Pc, in0=Pc, in1=den.unsqueeze(2).to_broadcast([16, 4, 128]), op=ALU.mult), r=[N_("Pc"), N_("st_")], w=[N_("Pc")])
                    V(lambda e: e.tensor_tensor(out=Pn, in0=Pn, in1=den.unsqueeze(2).to_broadcast([16, 4, 4]), op=ALU.mult), r=[N_("Pn"), N_("st_")], w=[N_("Pn")])
            def sa_s2(b):
                    par = b % 2
                    scc, scn, Pc, Pn, PcT, PnT, ob, st_ = SAB[par]
                    vnfl, vnbl = vn_[par]
                    negm, m2, rs1, rs2, es_, den = (st_[:, 0:4], st_[:, 4:8], st_[:, 8:12], st_[:, 12:16], st_[:, 16:20], st_[:, 20:24])
                    N_ = lambda x: "%s_%d" % (x, par)
                    pt, pn = next_pt()
                    for kvh in range(4):
                        S.op("pe", lambda e, pt=pt, kvh=kvh: e.transpose(pt[:, kvh, 0:16], Pc[:, kvh, :], ident[:16, :16]),
                             reads=[N_("Pc"), "ident"], writes=[pn], inc=False, mode="Tk32")
                        S.op("pe", lambda e, pt=pt, kvh=kvh: e.transpose(pt[0:4, 4 + kvh, 0:16], Pn[:, kvh, :], ident[:16, :16]),
                             reads=[N_("Pn"), "ident"], writes=[pn], inc=(kvh == 3), mode="Tk32b")
                    A_(lambda e, pt=pt: e.copy(out=PcT, in_=pt[:, 0:4, 0:16]), r=[pn], w=[N_("PcT")])
                    A_(lambda e, pt=pt: e.copy(out=PnT, in_=pt[0:4, 4:8, 0:16]), r=[pn], w=[N_("PnT")])
                    pmo, pon = next_ps()
                    for kvh in range(4):
                        S.op("pe", lambda e, pmo=pmo, kvh=kvh, b=b: e.matmul(
                            pmo[:16, kvh * 64:(kvh + 1) * 64], PcT[:, kvh, :], kcv[:, 1, b, kvh * 64:(kvh + 1) * 64], start=True, stop=False),
                            reads=[N_("PcT"), kvn], writes=[pon], inc=False, mode="m32")
                        S.op("pe", lambda e, pmo=pmo, kvh=kvh: e.matmul(
                            pmo[:16, kvh * 64:(kvh + 1) * 64], PnT[:, kvh, :], vnbl[:, kvh * 64:(kvh + 1) * 64], start=False, stop=True),
                            reads=[N_("PnT"), N_("vnb")], writes=[pon], inc=(kvh == 3), mode="k32m32")
                    A_(lambda e, pmo=pmo: e.copy(out=ob, in_=pmo[:16, 0:256]), r=[pon], w=[N_("ob")])
                    pt, pn = next_pt()
                    for a in range(2):
                        S.op("pe", lambda e, pt=pt, a=a: e.transpose(pt[:, a, 0:16], ob[:, a * 128:(a + 1) * 128], ident[:16, :16]),
                             reads=[N_("ob"), "ident"], writes=[pn], inc=(a == 1), mode="Tk32")
                    for a in range(2):
                        A_(lambda e, pt=pt, a=a, b=b: e.copy(out=oT[:, 4 * a:4 * a + 4, 4 * b:4 * b + 4],
                                                            in_=pt[:, a, 0:16].rearrange("p (i t) -> p i t", i=4)), r=[pn], w=["oT"])
            sa_s1(0)
            for b in range(16):
                if b + 1 < 16:
                    sa_s1(b + 1)
                sa_s2(b)
            S.barrier()

        if ENABLE_SSM:
            def pre_A(pt_i):
                S.dma("sp", x_tok[:, :, :], xpre[pt_i * 512:(pt_i + 1) * 512, :].rearrange("(j p) d -> p j d", p=128), writes=["x_tok"])
                rmsnorm_to_T(x_tok, "x_tok", gm, "gm", 4, 128)

            def pre_B():
                ssm_u(512, bf=False)
                (ssm_relayout4() if RELAYOUT4 else ssm_relayout(64, list(range(8)), lambda r: slice(r, 512, 8)))

            if NPRE > 0:
                pre_A(0)
                pre_B()
                if NPRE > 1:
                    pre_A(1)
                ssm_S(64)
            for pt_i in range(NPRE):
                if pt_i + 1 < NPRE:
                    pre_B()
                if pt_i + 2 < NPRE:
                    pre_A(pt_i + 2)
                ssm_reduce(0)
                if pt_i + 1 < NPRE:
                    ssm_S(64)
                ssm_reduce(1)
            phase_barrier()

        S.dma("sp", x_tok[:, 0, :], xh, writes=["x_tok"])
        rmsnorm_to_T(x_tok, "x_tok", gm, "gm", 1, 128)
        wkv, wkv_n = load_w(w_in_v, 8, 1024, 512)
        kv_token_major(wkv, wkv_n, 0, 128, 0)
        wkd, wkd_n = load_w(w_in_v, 8, 1024, 256, dup_heads=True)
        k_feature_major(wkd, wkd_n, 128, 0)

        def front_b(kind, t0, nt, nsub, psz):
            wkv, wkv_n = load_w(w_in_v, 8, 1024, 512)
            for j in range(nsub):
                kv_token_major(wkv, wkv_n, j, psz, (j + 1) if kind == "p" else None)
            if kind == "p" and t0 + nt == TP:
                S.dma("sp", kp, kv_tok[:, 3, 0:256], reads=["kv_tok"], sem_key="o_kvp")
                S.dma("sp", vp, kv_tok[:, 3, 256:512], reads=["kv_tok"], sem_key="o_kvp")
            if kind == "s":
                for bb in range(16):
                    S.dma("sp", ks[bb, 124:128, :], kv_tok[4 * bb:4 * bb + 4, 0, 0:256], reads=["kv_tok"], sem_key="o_kvs")
                    S.dma("sp", vs[bb, 124:128, :], kv_tok[4 * bb:4 * bb + 4, 0, 256:512], reads=["kv_tok"], sem_key="o_kvs")
            if kind == "p":
                wkd, wkd_n = load_w(w_in_v, 8, 1024, 256, dup_heads=True)
                k_feature_major(wkd, wkd_n, nt, 128)
                wq, wq_n = load_w(w_in_v, 8, 0, 1024)
                for m in range(8):
                    pm, pmn = fm_proj(wq, wq_n, m, hT, "hT", nt)
                    A_(lambda e, m=m, pm=pm: e.activation(out=qT[:, m, :nt], in_=pm[:, :nt], func=AF.Copy, scale=0.125),
                       r=[pmn], w=["qT", "uT"])

        tiles = [("p", t * 512, 512) for t in range(int(os.environ.get("NPT", "4")))] + [("s", 0, 64)]
        for ti, (kind, t0, nt) in enumerate(tiles):
            nsub = max(1, nt // 128)
            psz = min(128, nt)
            xt, xname = x_tok, "x_tok"
            src = xp if kind == "p" else xs
            ydst = yp if kind == "p" else ys
            S.dma("sp", xt[:psz, :nsub, :], src[t0:t0 + nt, :].rearrange("(j p) d -> p j d", p=psz), writes=[xname])
            rmsnorm_to_T(xt, xname, gm, "gm", nsub, psz)

            if ENABLE_SSM:
                ssm_u(nt)
                if kind == "p":
                    (ssm_relayout4() if RELAYOUT4 else ssm_relayout(64, list(range(8)), lambda r: slice(r, 512, 8)))
                    ssm_S(64)
                    front_b(kind, t0, nt, nsub, psz)
                    ssm_recur(64, True)
                    ssm_Y(64)
                    ssm_back(nt, 64, list(range(8)))
                    if t0 + nt == TP:
                        state_out(X[:, 0, :], ["X"], 64, st_p)
                else:
                    ssm_relayout(16, [4, 5, 6, 7], lambda r: slice(r - 4, 64, 4))
                    ssm_S(16)
                    HH = SSx[:, 16:48, :, :].rearrange("p n s g -> p (n s g)")[:, 0:2048].rearrange("p (s b g) -> p s b g", s=2, b=16)
                    TT = SSx[:, 16:48, :, :].rearrange("p n s g -> p (n s g)")[:, 2048:4096].rearrange("p (s b g) -> p s b g", s=2, b=16)
                    H0 = SSx[:, 48:64, :, :].rearrange("p n s g -> p (n s g)")[:, 0:2048].rearrange("p (s b g) -> p s b g", s=2, b=16)
                    S.dma("sp", H0[:, 0], h0hh_in, writes=["H0"], sem_key="h0")
                    S.dma("sp", H0[:, 1], h0hs_in, writes=["H0"], sem_key="h0")
                    bcb = lambda t, s_: t[:, s_, :].unsqueeze(1).to_broadcast([128, 16, 64])
                    for s_ in range(2):
                        V(lambda e, s_=s_: e.tensor_tensor(out=HH[:, s_], in0=H0[:, s_], in1=bcb(PR2, s_), op=ALU.mult),
                          r=["H0", "PR2"], w=["HH"])
                        V(lambda e, s_=s_: e.tensor_tensor(out=TT[:, s_], in0=H0[:, 1 - s_], in1=bcb(PI2, s_), op=ALU.mult),
                          r=["H0", "PI2"], w=["TT"])
                    for s_ in range(2):
                        V(lambda e, s_=s_: e.tensor_tensor(out=HH[:, s_], in0=HH[:, s_], in1=TT[:, s_], op=ALU.add),
                          r=["HH", "TT"], w=["HH"])
                    A_(lambda e: e.copy(out=Hp[:, :, 0:16], in_=HH[:, 0].rearrange("p b g -> p g b")), r=["HH"], w=["Hp"])
                    ssm_Y(16)
                    ssm_back(nt, 16, [4, 5, 6, 7])
                    V(lambda e: e.tensor_tensor(out=H0[:, 0], in0=HH[:, 0], in1=bcb(AR2, 0), op=ALU.mult), r=["HH", "AR2"], w=["H0"])
                    V(lambda e: e.tensor_tensor(out=H0[:, 1], in0=HH[:, 1], in1=bcb(AI2, 0), op=ALU.mult), r=["HH", "AI2"], w=["H0"])
                    V(lambda e: e.tensor_tensor(out=H0[:, 0], in0=H0[:, 0], in1=H0[:, 1], op=ALU.add), r=["H0"], w=["H0"])
                    V(lambda e: e.tensor_tensor(out=H0[:, 0], in0=H0[:, 0], in1=SSx[:, 0:16, 0, :], op=ALU.add), r=["H0", "SSx"], w=["H0"])
                    state_out(H0[:, 0].rearrange("p b g -> p (b g)"), ["H0"], 1024, st_s)
                phase_barrier()

            if not (ENABLE_SSM and kind == "p"):
                front_b(kind, t0, nt, nsub, psz)
            if kind == "s" and not NO_ATTN:
                sample_attention()
            elif kind == "p" and not NO_ATTN:
                units = [(j, kvh) for j in range(4) for kvh in range(4)]
                def s1(u):
                    j, kvh = units[u]
                    first = (ti == 0 and j == 0)
                    attention_s1(j, kvh, m0_t if first else mA_t, "m0_t" if first else "mA_t", u % 2)
                s1(0)
                for u in range(16):
                    if u + 1 < 16:
                        s1(u + 1)
                    attention_s2(units[u][0], units[u][1], u % 2)
                A_(lambda e: e.copy(out=kT2[:, :, 0:128], in_=kT2[:, :, 512:640]), r=["kT2"], w=["kT2"])
                A_(lambda e: e.copy(out=vpad[:, 0], in_=vpad[:, 4]), r=["vpad"], w=["vpad"])
            else:
                V(lambda e: e.memset(oT[:, :, :nt], 0.0), w=["oT"])

            wg, wg_n = load_w(w_in_v, 8, 2560, 1024)
            for m in range(8):
                pm, pmn = fm_proj(wg, wg_n, m, hT, "hT", nt)
                A_(lambda e, m=m, pm=pm: e.activation(out=mA[:, m, :nt], in_=pm[:, :nt], func=AF.Sigmoid), r=[pmn], w=["mA"])
            if kind == "s":
                slot, wa_n = next_slot()
                wa = slot[:, :].rearrange("p (k m) -> p k m", k=8)
                wao_h = (wbf["w_ao"] if WCONV else w_ao).rearrange("(h d) m -> d h m", d=64)
                for a in range(2):
                    for kl in range(2):
                        S.dma("pool", wa[kl * 64:(kl + 1) * 64, 4 * a:4 * a + 4, :],
                              wao_h[:, 8 * a + 4 * kl:8 * a + 4 * kl + 4, :], reads=(["wb_w_ao"] if WCONV else []), writes=[wa_n])
            else:
                wa, wa_n = load_w(w_ao_v, 8, 0, 1024)
            for m in range(8):
                pm, pmn = fm_proj(wa, wa_n, m, oT, "oT", nt)
                V(lambda e, m=m, pm=pm: e.tensor_tensor(out=mA[:, m, :nt], in0=pm[:, :nt], in1=mA[:, m, :nt], op=ALU.mult),
                  r=[pmn, "mA"], w=["mA"])
            if ENABLE_SSM:
                wg, wg_n = load_w(w_in_v, 8, 3584, 1024)
                for m in range(8):
                    pm, pmn = fm_proj(wg, wg_n, m, hT, "hT", nt)
                    A_(lambda e, m=m, pm=pm: e.activation(out=mB[:, m, :nt], in_=pm[:, :nt], func=AF.Sigmoid), r=[pmn], w=["mB"])
                wb, wb_n = load_w(w_gl_v, 8, 1024, 1024)
                for m in range(8):
                    pm, pmn = fm_proj(wb, wb_n, m, gyT, "gyT", nt)
                    A_(lambda e, pm=pm: e.activation(out=rtmp[:, :nt], in_=pm[:, :nt], func=AF.Sigmoid), r=[pmn], w=["rtmp"])
                    V(lambda e, m=m: e.tensor_tensor(out=mB[:, m, :nt], in0=mB[:, m, :nt], in1=rtmp[:, :nt], op=ALU.mult),
                      r=["mB", "rtmp"], w=["mB"])
                wa2, wa2_n = load_w(w_gl_v, 8, 0, 1024)
                for m in range(8):
                    pm, pmn = fm_proj(wa2, wa2_n, m, gyT, "gyT", nt)
                    V(lambda e, m=m, pm=pm: e.tensor_tensor(out=rtmp[:, :nt], in0=pm[:, :nt], in1=mB[:, m, :nt], op=ALU.mult),
                      r=[pmn, "mB"], w=["rtmp"])
                    V(lambda e, m=m: e.tensor_tensor(out=mA[:, m, :nt], in0=mA[:, m, :nt], in1=rtmp[:, :nt], op=ALU.add),
                      r=["mA", "rtmp"], w=["mA"])
            wo, wo_n = load_w(w_o_v, 8, 0, 1024)
            S.pipe = "G" in PIPE
            for j in range(nsub):
                for cb in range(2):
                    pm, pmn = next_ps()
                    for k in range(8):
                        S.op("pe", lambda e, k=k, j=j, cb=cb, pm=pm: e.matmul(
                            pm[:psz, :], mA[:, k, j * 128:j * 128 + psz], wo[:, k, cb * 512:(cb + 1) * 512],
                            start=(k == 0), stop=(k == 7)), reads=["mA", wo_n], writes=[pmn], inc=(k == 7), mode=("full" if psz == 128 else "m64"))
                    V(lambda e, j=j, cb=cb, pm=pm: e.tensor_tensor(
                        out=xt[:psz, j, cb * 512:(cb + 1) * 512], in0=pm[:psz, :], in1=xt[:psz, j, cb * 512:(cb + 1) * 512],
                        op=ALU.add), r=[pmn, xname], w=[xname])

            S.pipe = False
            rmsnorm_to_T(xt, xname, gf, "gf", nsub, psz)
            for half in range(2):
                for cb in range(2):
                    wv, wn = load_w(w_up_v, 8, half * 2048 + cb * 1024, 1024)
                    for m in range(8):
                        pm, pmn = fm_proj(wv, wn, m, hT, "hT", nt)
                        f = cb * 8 + m
                        A_(lambda e, pm=pm: e.activation(out=rtmp[:, :nt], in_=pm[:, :nt], func=AF.Relu), r=[pmn], w=["rtmp"])
                        V(lambda e, f=f: e.tensor_tensor(out=aT[:, f, :nt], in0=rtmp[:, :nt], in1=rtmp[:, :nt], op=ALU.mult),
                          r=["rtmp"], w=["aT"])
                for cb in range(2):
                    wv, wn = load_w(w_down_v, 16, cb * 512, 512, k0=16 * half)
                    S.pipe = "G" in PIPE
                    for j in range(nsub):
                        pm, pmn = next_ps()
                        for k in range(16):
                            S.op("pe", lambda e, k=k, j=j, pm=pm, wv=wv: e.matmul(
                                pm[:psz, :], aT[:, k, j * 128:j * 128 + psz], wv[:, k, :],
                                start=(k == 0), stop=(k == 15)), reads=["aT", wn], writes=[pmn], inc=(k == 15), mode=("full" if psz == 128 else "m64"))
                        V(lambda e, j=j, cb=cb, pm=pm: e.tensor_tensor(
                            out=xt[:psz, j, cb * 512:(cb + 1) * 512], in0=pm[:psz, :],
                            in1=xt[:psz, j, cb * 512:(cb + 1) * 512], op=ALU.add), r=[pmn, xname], w=[xname])
            S.pipe = False
            rms_stats(xt, xname, nsub, psz)
            for j in range(nsub):
                V(lambda e, j=j: e.scalar_tensor_tensor(out=xt[:psz, j, :], in0=xt[:psz, j, :], scalar=rstd[:psz, j:j + 1],
                                                        in1=gl[:psz, :], op0=ALU.mult, op1=ALU.mult),
                  r=[xname, "rstd", "gl"], w=[xname])
            S.dma("sp", ydst[t0:t0 + nt, :].rearrange("(j p) d -> p j d", p=psz), xt[:psz, :nsub, :],
                  reads=[xname], sem_key="o_y")
            if ENABLE_SSM:
                phase_barrier()
        S.finish()
    return nc


_NC = None


def _masks(first_chunk):
    qi = np.arange(128)[:, None]
    si = np.arange(256)[None, :]
    diff = qi + 128 - si
    band = (diff >= 0) & (diff < 128)
    m_a = np.where(band, 0.0, NEG).astype(np.float32)
    m_0 = m_a.copy()
    if first_chunk:
        m_0[:, :128] = NEG
    return m_a, m_0


def _structural_constants():
    selc = np.zeros((128, 64, 128), np.float32)
    selT = np.zeros((128, 64, 128), np.float32)
    c = np.arange(16)
    for gl_ in range(8):
        for r in range(8):
            selc[gl_ * 16 + c, gl_ * 8 + r, r * 16 + c] = 1.0
            selT[r * 16 + c, gl_ * 8 + r, gl_ * 16 + c] = 1.0
    rr = np.arange(128) // 16
    bmask = (rr[:, None] <= rr[None, :]).astype(np.float32)
    r = np.arange(8)
    kvec = np.concatenate([7 - r, -r, r + 1, r, [1, 8, -4, 16, 32, 64, 128, 256, 512]]).astype(np.float32)
    t = np.repeat(np.arange(4)[None, :], 4, 0).reshape(16)
    msc = np.where(np.arange(128)[None, :] >= t[:, None] + 1, 0.0, NEG).astype(np.float32)
    msn = np.where(np.arange(4)[None, :] <= t[:, None], 0.0, NEG).astype(np.float32)
    return selc, selT, bmask, np.tile(kvec[None], (128, 1)), msc, msn


def kernel(x_prompt, x_sample, cache_k, cache_v, state_ssm_re, state_ssm_im, g_mix, w_in,
           attn_sinks, w_attn_o, ssm_lambda_re, ssm_lambda_im, ssm_log_dt, ssm_b_re, ssm_b_im,
           ssm_c_re, ssm_c_im, ssm_d, w_glu, w_out, g_ffn, w_up, w_down, g_final):
    global _NC
    f = lambda a: np.ascontiguousarray(np.asarray(a, dtype=np.float32))
    x_prompt, x_sample = f(x_prompt), f(x_sample)
    if _NC is None:
        _NC = build_nc()
    selc, selT, bmask, kvec, msc, msn = _structural_constants()
    sink_s = f(np.repeat(f(attn_sinks).reshape(4, 4).T, 4, axis=0))
    dup = lambda a: f(np.concatenate([a, a], 0))
    lamT_re, lamT_im = dup(f(ssm_lambda_re)[0].T), dup(f(ssm_lambda_im)[0].T)
    ldt = f(np.tile(f(ssm_log_dt)[0][None, :], (128, 1)))
    bT_re, bT_im = dup(f(ssm_b_re)[0].transpose(1, 0, 2)), dup(f(ssm_b_im)[0].transpose(1, 0, 2))
    cT_re, cT_im = dup(f(ssm_c_re)[0].transpose(2, 0, 1)), dup(f(ssm_c_im)[0].transpose(2, 0, 1))
    dT = f(f(ssm_d)[0].reshape(8, 128).T)
    dd = f(np.tile(f(ssm_d)[0].reshape(64, 16).T, (8, 1)))
    sre, sim_ = f(state_ssm_re)[0], f(state_ssm_im)[0]
    npre_rows = max(NPRE, 1) * 512
    in_maps = []
    for c in range(NCORES):
        b, k = c // 4, c % 4
        m_a, m_0 = _masks(k == 0)
        xh = x_prompt[b, k * TP - 128:k * TP] if k > 0 else np.zeros((128, D), np.float32)
        xpre = np.zeros((npre_rows, D), np.float32)
        if k > 0 and NPRE > 0:
            xpre[npre_rows - k * TP:] = x_prompt[b, 0:k * TP]
        hre = sre[16 * c:16 * c + 16].transpose(2, 0, 1)
        him = sim_[16 * c:16 * c + 16].transpose(2, 0, 1)
        in_maps.append({
            "xp": f(x_prompt[b, k * TP:(k + 1) * TP]), "xh": f(xh), "xpre": xpre,
            "xs": f(x_sample[16 * c:16 * c + 16].reshape(TS, D)),
            "ck": f(np.asarray(cache_k)[0, 16 * c:16 * c + 16].reshape(16, 128, 256)),
            "cv": f(np.asarray(cache_v)[0, 16 * c:16 * c + 16].reshape(16, 128, 256)),
            "g_mix": f(g_mix).reshape(1, D), "g_ffn": f(g_ffn).reshape(1, D), "g_fin": f(g_final).reshape(1, D),
            "sinks": f(attn_sinks).reshape(1, 16), "mask_a": m_a, "mask_0": m_0,
            "w_in": f(w_in)[0], "w_ao": f(w_attn_o)[0], "w_gl": f(w_glu)[0], "w_o": f(w_out)[0],
            "w_up": f(w_up)[0], "w_down": f(w_down)[0],
            "lamT_re": lamT_re, "lamT_im": lamT_im, "ldt": ldt, "bT_re": bT_re, "bT_im": bT_im,
            "cT_re": cT_re, "cT_im": cT_im, "dT": dT, "dd": dd, "kvec": kvec, "selc": selc, "selTc": selT, "bmask": bmask,
            "h0hh": f(np.concatenate([hre, him], 0)), "h0hs": f(np.concatenate([him, hre], 0)),
            "mask_sc": msc, "mask_sn": msn, "sink_sx": sink_s,
        })
    res = run_bass_kernel_spmd(_NC, in_maps, core_ids=list(range(NCORES)))
    R = res.results
    y_prompt = np.stack([np.concatenate([R[b * 4 + k]["yp"] for k in range(4)], 0) for b in range(2)], 0)
    y_sample = np.concatenate([R[c]["ys"].reshape(16, 4, D) for c in range(NCORES)], 0)
    k_prompt = np.stack([R[b * 4 + 3]["kp"].reshape(128, 4, 64) for b in range(2)], 0)[None]
    v_prompt = np.stack([R[b * 4 + 3]["vp"].reshape(128, 4, 64) for b in range(2)], 0)[None]
    k_sample = np.concatenate([R[c]["ks"].reshape(16, 128, 4, 64) for c in range(NCORES)], 0)[None]
    v_sample = np.concatenate([R[c]["vs"].reshape(16, 128, 4, 64) for c in range(NCORES)], 0)[None]
    ssm_re_p = np.stack([R[b * 4 + 3]["st_p"][:, 0:64] for b in range(2)], 0)[None]
    ssm_im_p = np.stack([R[b * 4 + 3]["st_p"][:, 64:128] for b in range(2)], 0)[None]
    ssm_re_s = np.concatenate([R[c]["st_s"].reshape(16, 64, 128)[:, :, 0:64] for c in range(NCORES)], 0)[None]
    ssm_im_s = np.concatenate([R[c]["st_s"].reshape(16, 64, 128)[:, :, 64:128] for c in range(NCORES)], 0)[None]
    asf = lambda a: np.ascontiguousarray(a, dtype=np.float32)
    return (asf(y_prompt), asf(y_sample), asf(k_prompt), asf(v_prompt), asf(ssm_re_p), asf(ssm_im_p),
            asf(k_sample), asf(v_sample), asf(ssm_re_s), asf(ssm_im_s))
```

```python
import numpy as np
from contextlib import ExitStack
import concourse.bass as bass
import concourse.mybir as mybir
from concourse.bass_utils import run_bass_kernel_spmd

F32 = mybir.dt.float32
BF16 = mybir.dt.bfloat16
AF = mybir.ActivationFunctionType
ALU = mybir.AluOpType
AX = mybir.AxisListType

D = 1024
D_IN = 4608
D_FF = 4096
NCORES = 8
TP = 2048
TS = 64
EPS = 1e-5
NEG = -30000.0
ENABLE_SSM = True
import os
NO_ATTN = bool(int(os.environ.get("NO_ATTN", "0")))
DENSE_INC = bool(int(os.environ.get("DENSE_INC", "1")))
PIPE = os.environ.get("PIPE", "ABCDEFGHI")
RELAYOUT4 = bool(int(os.environ.get("RELAYOUT4", "1")))
WCONV = bool(int(os.environ.get("WCONV", "0")))


class Sync:
    def __init__(self, nc, es):
        self.nc = nc
        self.eng = {"pe": nc.tensor, "act": nc.scalar, "dve": nc.vector, "pool": nc.gpsimd, "sp": nc.sync}
        self.sem = {k: es.enter_context(nc.semaphore("s_" + k)) for k in ("pe", "act", "dve")}
        self.cnt = {k: 0 for k in self.sem}
        self.seen = {}
        self.wr = {}
        self.rd = {}
        self.es = es
        self.dma_sems = {}
        self.pipe = False

    def _wait(self, e, tok):
        s, v = tok
        key = (e, id(s))
        if self.seen.get(key, 0) >= v:
            return
        if e in self.sem and s is self.sem[e] and ((e == "pe" and self.pipe) or v > self.cnt[e]):
            return
        self.eng[e].wait_ge(s, v)
        self.seen[key] = v

    def deps(self, e, reads, writes):
        for b in reads:
            for t in self.wr.get(b, {}).values():
                self._wait(e, t)
        for b in writes:
            for t in self.wr.get(b, {}).values():
                self._wait(e, t)
            for t in self.rd.get(b, {}).values():
                self._wait(e, t)

    def done(self, tok, reads, writes):
        for b in reads:
            self.rd.setdefault(b, {})[id(tok[0])] = tok
        for b in writes:
            self.wr[b] = {id(tok[0]): tok}
            self.rd[b] = {}

    def op(self, e, fn, reads=(), writes=(), inc=True, mode="full"):
        self.deps(e, reads, writes)
        if e == "pe" and mode != getattr(self, "pe_mode", "full"):
            if self.cnt["pe"] > 0:
                self.eng["pe"].wait_ge(self.sem["pe"], self.cnt["pe"])
            self.pe_mode = mode
        ins = fn(self.eng[e])
        if inc or DENSE_INC:
            self.cnt[e] += 1
            ins.then_inc(self.sem[e], 1)
            self.done((self.sem[e], self.cnt[e]), reads, writes)
        else:
            self.done((self.sem[e], self.cnt[e] + 1), reads, writes)

    def dma(self, q, out, in_, reads=(), writes=(), sem_key=None):
        key = sem_key if sem_key is not None else (writes[0] if writes else reads[0])
        if key not in self.dma_sems:
            self.dma_sems[key] = [self.es.enter_context(self.nc.semaphore("d_%d" % len(self.dma_sems))), 0]
        self.deps(q, reads, writes)
        rec = self.dma_sems[key]
        rec[1] += 16
        self.eng[q].dma_start(out=out, in_=in_).then_inc(rec[0], 16)
        self.done((rec[0], rec[1]), reads, writes)

    def barrier(self):
        toks = [(self.sem[k], self.cnt[k]) for k in self.sem if self.cnt[k] > 0]
        toks += [(s, v) for (s, v) in self.dma_sems.values() if v > 0]
        for e in self.eng:
            for t in toks:
                self._wait(e, t)

    def finish(self):
        for key, (s, v) in self.dma_sems.items():
            if v:
                self.nc.sync.wait_ge(s, v)


NPRE = int(os.environ.get("NPRE", "12"))
TWO_PI = 6.283185307179586


def build_nc():
    nc = bass.Bass("TRN2", target_bir_lowering=False)
    dt_in = lambda n, s: nc.dram_tensor(n, s, F32, kind="ExternalInput").ap()
    dt_out = lambda n, s: nc.dram_tensor(n, s, F32, kind="ExternalOutput").ap()
    xp = dt_in("xp", [TP, D])
    xh = dt_in("xh", [128, D])
    xpre = dt_in("xpre", [max(NPRE, 1) * 512, D])
    xs = dt_in("xs", [TS, D])
    ck = dt_in("ck", [16, 128, 256])
    cv = dt_in("cv", [16, 128, 256])
    g_mix = dt_in("g_mix", [1, D])
    g_ffn = dt_in("g_ffn", [1, D])
    g_fin = dt_in("g_fin", [1, D])
    sinks = dt_in("sinks", [1, 16])
    mask_a = dt_in("mask_a", [128, 256])
    mask_0 = dt_in("mask_0", [128, 256])
    w_in = dt_in("w_in", [D, D_IN])
    w_ao = dt_in("w_ao", [D, D])
    w_gl = dt_in("w_gl", [D, 2 * D])
    w_o = dt_in("w_o", [D, D])
    w_up = dt_in("w_up", [D, D_FF])
    w_down = dt_in("w_down", [D_FF, D])
    lamT_re = dt_in("lamT_re", [128, 64]); lamT_im = dt_in("lamT_im", [128, 64]); ldt_in = dt_in("ldt", [128, 64])
    bT_re = dt_in("bT_re", [128, 64, 16]); bT_im = dt_in("bT_im", [128, 64, 16])
    cT_re = dt_in("cT_re", [128, 64, 16]); cT_im = dt_in("cT_im", [128, 64, 16])
    dT_in = dt_in("dT", [128, 8])
    dd_in = dt_in("dd", [128, 64])
    kvec_in = dt_in("kvec", [128, 41])
    selc = dt_in("selc", [128, 64, 128]); selTc = dt_in("selTc", [128, 64, 128]); bmask_in = dt_in("bmask", [128, 128])
    h0hh_in = dt_in("h0hh", [128, 16, 64]); h0hs_in = dt_in("h0hs", [128, 16, 64])
    msc_in = dt_in("mask_sc", [16, 128]); msn_in = dt_in("mask_sn", [16, 4]); sink_s_in = dt_in("sink_sx", [16, 4])
    yp = dt_out("yp", [TP, D])
    ys = dt_out("ys", [TS, D])
    kp = dt_out("kp", [128, 256])
    vp = dt_out("vp", [128, 256])
    ks = dt_out("ks", [16, 128, 256])
    vs = dt_out("vs", [16, 128, 256])
    st_p = dt_out("st_p", [64, 128])
    st_s = dt_out("st_s", [1024, 128])
    tb = lambda n: nc.dram_tensor(n, [128, 64, 128], BF16).ap()
    RT_d, RTs_d, Toep_d, Om_d = tb("RT_d"), tb("RTs_d"), tb("Toep_d"), tb("Om_d")

    kview = lambda w: w.rearrange("(k p) m -> p k m", p=128)
    wbf = {n: nc.dram_tensor(n + "_bf", list(shp), BF16).ap() for n, shp in
           (("w_in", (D, D_IN)), ("w_ao", (D, D)), ("w_gl", (D, 2 * D)), ("w_o", (D, D)), ("w_up", (D, D_FF)), ("w_down", (D_FF, D)))}
    w_in_v, w_ao_v, w_gl_v, w_o_v, w_up_v, w_down_v = (kview(w_in), kview(w_ao), kview(w_gl), kview(w_o),
                                                      kview(w_up), kview(w_down))
    BFV = {id_: (kview(wbf[n]), "wb_" + n) for id_, n in ((0, "w_in"), (1, "w_ao"), (2, "w_gl"), (3, "w_o"), (4, "w_up"), (5, "w_down"))}
    SRCV = {0: w_in_v, 1: w_ao_v, 2: w_gl_v, 3: w_o_v, 4: w_up_v, 5: w_down_v}

    with ExitStack() as es:
        S = Sync(nc, es)
        sb = lambda n, s, d: es.enter_context(nc.sbuf_tensor(n, s, d))
        ident = sb("ident", [128, 128], BF16)
        psw = sb("psw", [128, 128], BF16)
        identf = sb("identf", [128, 128], F32)
        gm = sb("gm", [128, D], F32)
        gf = sb("gf", [128, D], F32)
        gl = sb("gl", [128, D], F32)
        sink_bc = sb("sink_bc", [128, 16], F32)
        nsink_bc = sb("nsink_bc", [128, 16], F32)
        mA_t = sb("mA_t", [128, 256], F32)
        m0_t = sb("m0_t", [128, 256], F32)
        AR2 = sb("AR2", [128, 2, 64], F32)
        AI2 = sb("AI2", [128, 2, 64], F32)
        PR2 = sb("PR2", [128, 2, 64], F32)
        TR = sb("TR", [128, 7, 64], F32)
        TI = sb("TI", [128, 7, 2, 64], F32)
        PI2 = sb("PI2", [128, 2, 64], F32)
        Dt = sb("Dt", [128, 8], F32)
        X = sb("X", [128, 3, 64], F32)
        t1 = sb("t1", [128, 2, 64], F32)
        t2 = sb("t2", [128, 2, 64], F32)
        ps_t = [es.enter_context(nc.psum_tensor("ps_t%d" % i, [128, 8, 128], BF16)) for i in range(2)]
        ps_m = [es.enter_context(nc.psum_tensor("ps_m%d" % i, [128, 512], F32)) for i in range(6)]

        wctr = [0]

        def next_slot():
            i = wctr[0] % 2
            wctr[0] += 1
            return wslot[i], "wslot%d" % i

        def load_w(view, nk, c0, ncols, dup_heads=False, k0=0, bf=True):
            slot, name = next_slot()
            rd = []
            if WCONV and bf:
                wid = [k_ for k_, v_ in SRCV.items() if v_ is view][0]
                view, rdn = BFV[wid]
                rd = [rdn]
            if dup_heads:
                dst = slot[:, 0:nk * 512].rearrange("p (k h u d) -> p k h u d", k=nk, h=4, u=2)
                for h in range(4):
                    for u in range(2):
                        S.dma("pool", dst[:, :, h, u, :], view[:, 0:nk, c0 + h * 64:c0 + (h + 1) * 64], reads=rd, writes=[name], sem_key=name)
                return slot[:, 0:nk * 512].rearrange("p (k m) -> p k m", k=nk), name
            dst = slot[:, 0:nk * ncols].rearrange("p (k m) -> p k m", k=nk)
            S.dma("pool", dst, view[:, k0:k0 + nk, c0:c0 + ncols], reads=rd, writes=[name], sem_key=name)
            return dst, name

        def load_tab(src):
            slot, name = next_slot()
            dst = slot[:, :].rearrange("p (g m) -> p g m", g=64)
            for hh_ in range(2):
                S.dma("pool", dst[:, 32 * hh_:32 * hh_ + 32, :], src[:, 32 * hh_:32 * hh_ + 32, :], writes=[name])
            return dst, name

        pctr = [0]

        def next_ps():
            i = pctr[0] % 6
            pctr[0] += 1
            return ps_m[i], "ps_m%d" % i

        tctr = [0]

        def next_pt():
            i = tctr[0] % 2
            tctr[0] += 1
            return ps_t[i], "ps_t%d" % i

        V = lambda fn, r=(), w=(): S.op("dve", fn, reads=r, writes=w)
        A_ = lambda fn, r=(), w=(): S.op("act", fn, reads=r, writes=w)

        V(lambda e: e.memset(identf[:], 1.0), w=["identf"])
        nc.gpsimd.wait_ge(S.sem["dve"], S.cnt["dve"])
        pool_sem = es.enter_context(nc.semaphore("s_pool0"))
        nc.gpsimd.affine_select(out=identf[:], in_=identf[:], pattern=[[-1, 128]], compare_op=ALU.is_equal,
                                fill=0.0, base=0, channel_multiplier=1).then_inc(pool_sem, 1)
        S.wr["identf"] = {id(pool_sem): (pool_sem, 1)}
        V(lambda e: e.tensor_copy(ident[:], identf[:]), r=["identf"], w=["ident"])
        V(lambda e: e.tensor_copy(psw[:, 0:64], identf[:, 64:128]), r=["identf"], w=["psw"])
        V(lambda e: e.tensor_copy(psw[:, 64:128], identf[:, 0:64]), r=["identf"], w=["psw"])
        V(lambda e: e.memset(X[:], 0.0), w=["X"])
        S.dma("sp", gm[:], g_mix.partition_broadcast(128), writes=["gm"])
        S.dma("sp", gf[:], g_ffn.partition_broadcast(128), writes=["gf"])
        S.dma("sp", gl[:], g_fin.partition_broadcast(128), writes=["gl"])
        S.dma("sp", sink_bc[:], sinks.partition_broadcast(128), writes=["sink_bc"])
        S.dma("sp", mA_t[:], mask_a, writes=["mA_t"])
        S.dma("sp", m0_t[:], mask_0, writes=["m0_t"])
        S.dma("sp", Dt[:], dT_in, writes=["Dt"])
        V(lambda e: e.tensor_scalar(nsink_bc[:], sink_bc[:], -1.0, None, ALU.mult), r=["sink_bc"], w=["nsink_bc"])
        S.dma("sp", ks[:, 0:124, :], ck[:, 4:128, :], sem_key="o_kvs")
        S.dma("sp", vs[:, 0:124, :], cv[:, 4:128, :], sem_key="o_kvs")

        def ssm_setup():
            with ExitStack() as ses, ExitStack() as ses1:
                sbs = lambda n, s, d: ses.enter_context(nc.sbuf_tensor(n, s, d))
                sb1 = lambda n, s, d: ses1.enter_context(nc.sbuf_tensor(n, s, d))
                NK = 41
                X1 = sbs("X1", [128, 64, 16], F32); X2 = sbs("X2", [128, 64, 16], F32)
                Y1 = sbs("Y1", [128, 64, 16], F32); Y2 = sbs("Y2", [128, 64, 16], F32)
                PWr = sbs("PWr", [128, NK, 64], F32); PWi = sbs("PWi", [128, NK, 64], F32)
                bmk = sbs("bmk", [128, 128], F32)
                ddt = sbs("ddt", [128, 64], F32)
                lr = sb1("lr", [128, 64], F32); li = sb1("li", [128, 64], F32); dtt = sb1("dtt", [128, 64], F32)
                lrd = sb1("lrd", [128, 64], F32); lid = sb1("lid", [128, 64], F32)
                kv = sb1("kv", [128, NK], F32)
                bre = sb1("bre", [128, 64, 16], F32); bim = sb1("bim", [128, 64, 16], F32)
                cre = sb1("cre", [128, 64, 16], F32); cim = sb1("cim", [128, 64, 16], F32)
                bbr = sb1("bbr", [128, 64, 16], F32); bbi = sb1("bbi", [128, 64, 16], F32)
                ang = sb1("ang", [128, NK, 64], F32); mag = sb1("mag", [128, NK, 64], F32)
                tq = sb1("tq", [128, NK, 64], F32); tf = sb1("tf", [128, NK, 64], F32); mk = sb1("mk", [128, NK, 64], F32)
                ti = sb1("ti", [128, NK, 64], mybir.dt.int32)
                s1 = sb1("s1", [128, 64], F32); s2 = sb1("s2", [128, 64], F32); s3 = sb1("s3", [128, 64], F32)
                fre = sb1("fre", [128, 64], F32); fim = sb1("fim", [128, 64], F32)
                S.dma("sp", lr[:], lamT_re, writes=["lr"]); S.dma("sp", li[:], lamT_im, writes=["li"])
                S.dma("sp", dtt[:], ldt_in, writes=["dtt"]); S.dma("sp", kv[:], kvec_in, writes=["kv"])
                S.dma("sp", bre[:], bT_re, writes=["bre"]); S.dma("sp", bim[:], bT_im, writes=["bim"])
                S.dma("sp", cre[:], cT_re, writes=["cre"]); S.dma("sp", cim[:], cT_im, writes=["cim"])
                S.dma("sp", bmk[:], bmask_in, writes=["bmk"])
                S.dma("sp", ddt[:], dd_in, writes=["ddt"])
                A_(lambda e: e.activation(out=dtt[:], in_=dtt[:], func=AF.Exp), r=["dtt"], w=["dtt"])
                V(lambda e: e.tensor_tensor(out=lrd[:], in0=lr[:], in1=dtt[:], op=ALU.mult), r=["lr", "dtt"], w=["lrd"])
                V(lambda e: e.tensor_tensor(out=lid[:], in0=li[:], in1=dtt[:], op=ALU.mult), r=["li", "dtt"], w=["lid"])
                bck = lambda t: t[:].unsqueeze(1).to_broadcast([128, NK, 64])
                kvb = kv[:].unsqueeze(2).to_broadcast([128, NK, 64])
                V(lambda e: e.tensor_tensor(out=ang[:], in0=bck(lid), in1=kvb, op=ALU.mult), r=["lid", "kv"], w=["ang"])
                V(lambda e: e.tensor_tensor(out=mag[:], in0=bck(lrd), in1=kvb, op=ALU.mult), r=["lrd", "kv"], w=["mag"])
                A_(lambda e: e.activation(out=mag[:], in_=mag[:], func=AF.Exp), r=["mag"], w=["mag"])

                def sin_of(out, oname, phase):
                    V(lambda e: e.tensor_scalar(tq[:], ang[:], 1.0 / TWO_PI, phase, ALU.mult, ALU.add), r=["ang"], w=["tq"])
                    V(lambda e: e.tensor_copy(ti[:], tq[:]), r=["tq"], w=["ti"])
                    V(lambda e: e.tensor_copy(tf[:], ti[:]), r=["ti"], w=["tf"])
                    V(lambda e: e.tensor_tensor(out=tq[:], in0=tq[:], in1=tf[:], op=ALU.subtract), r=["tq", "tf"], w=["tq"])
                    V(lambda e: e.tensor_single_scalar(mk[:], tq[:], 0.5, ALU.is_gt), r=["tq"], w=["mk"])
                    V(lambda e: e.tensor_tensor(out=tq[:], in0=tq[:], in1=mk[:], op=ALU.subtract), r=["tq", "mk"], w=["tq"])
                    V(lambda e: e.tensor_single_scalar(mk[:], tq[:], -0.5, ALU.is_lt), r=["tq"], w=["mk"])
                    V(lambda e: e.tensor_tensor(out=tq[:], in0=tq[:], in1=mk[:], op=ALU.add), r=["tq", "mk"], w=["tq"])
                    A_(lambda e: e.activation(out=tf[:], in_=tq[:], func=AF.Sin, scale=TWO_PI), r=["tq"], w=["tf"])
                    V(lambda e: e.tensor_tensor(out=out[:], in0=tf[:], in1=mag[:], op=ALU.mult), r=["tf", "mag"], w=[oname])

                sin_of(PWi, "PWi", 0.0)
                sin_of(PWr, "PWr", 0.25)
                ar, ai = PWr[:, 32, :], PWi[:, 32, :]
                V(lambda e: e.tensor_scalar(s1[:], ar, -1.0, None, ALU.add), r=["PWr"], w=["s1"])
                V(lambda e: e.tensor_tensor(out=s2[:], in0=lr[:], in1=lr[:], op=ALU.mult), r=["lr"], w=["s2"])
                V(lambda e: e.tensor_tensor(out=s3[:], in0=li[:], in1=li[:], op=ALU.mult), r=["li"], w=["s3"])
                V(lambda e: e.tensor_tensor(out=s2[:], in0=s2[:], in1=s3[:], op=ALU.add), r=["s2", "s3"], w=["s2"])
                V(lambda e: e.reciprocal(s2[:], s2[:]), r=["s2"], w=["s2"])
                V(lambda e: e.tensor_tensor(out=fre[:], in0=s1[:], in1=lr[:], op=ALU.mult), r=["s1", "lr"], w=["fre"])
                V(lambda e: e.tensor_tensor(out=s3[:], in0=ai, in1=li[:], op=ALU.mult), r=["PWi", "li"], w=["s3"])
                V(lambda e: e.tensor_tensor(out=fre[:], in0=fre[:], in1=s3[:], op=ALU.add), r=["fre", "s3"], w=["fre"])
                V(lambda e: e.tensor_tensor(out=fre[:], in0=fre[:], in1=s2[:], op=ALU.mult), r=["fre", "s2"], w=["fre"])
                V(lambda e: e.tensor_tensor(out=fim[:], in0=ai, in1=lr[:], op=ALU.mult), r=["PWi", "lr"], w=["fim"])
                V(lambda e: e.tensor_tensor(out=s3[:], in0=s1[:], in1=li[:], op=ALU.mult), r=["s1", "li"], w=["s3"])
                V(lambda e: e.tensor_tensor(out=fim[:], in0=fim[:], in1=s3[:], op=ALU.subtract), r=["fim", "s3"], w=["fim"])
                V(lambda e: e.tensor_tensor(out=fim[:], in0=fim[:], in1=s2[:], op=ALU.mult), r=["fim", "s2"], w=["fim"])
                fb = lambda t: t[:].unsqueeze(2).to_broadcast([128, 64, 16])
                V(lambda e: e.tensor_tensor(out=bbr[:], in0=bre[:], in1=fb(fre), op=ALU.mult), r=["bre", "fre"], w=["bbr"])
                V(lambda e: e.tensor_tensor(out=X1[:], in0=bim[:], in1=fb(fim), op=ALU.mult), r=["bim", "fim"], w=["X1"])
                V(lambda e: e.tensor_tensor(out=bbr[:], in0=bbr[:], in1=X1[:], op=ALU.subtract), r=["bbr", "X1"], w=["bbr"])
                V(lambda e: e.tensor_tensor(out=bbi[:], in0=bim[:], in1=fb(fre), op=ALU.mult), r=["bim", "fre"], w=["bbi"])
                V(lambda e: e.tensor_tensor(out=X1[:], in0=bre[:], in1=fb(fim), op=ALU.mult), r=["bre", "fim"], w=["X1"])
                V(lambda e: e.tensor_tensor(out=bbi[:], in0=bbi[:], in1=X1[:], op=ALU.add), r=["bbi", "X1"], w=["bbi"])
                lo, hi = slice(0, 64), slice(64, 128)
                V(lambda e: e.tensor_copy(X1[lo], bbr[lo]), r=["bbr"], w=["X1"])
                V(lambda e: e.tensor_copy(X1[hi], bbi[hi]), r=["bbi"], w=["X1"])
                V(lambda e: e.tensor_scalar(X2[lo], bbi[lo], -1.0, None, ALU.mult), r=["bbi"], w=["X2"])
                V(lambda e: e.tensor_copy(X2[hi], bbr[hi]), r=["bbr"], w=["X2"])
                V(lambda e: e.tensor_copy(Y1[lo], cre[lo]), r=["cre"], w=["Y1"])
                V(lambda e: e.tensor_scalar(Y1[hi], cim[hi], -1.0, None, ALU.mult), r=["cim"], w=["Y1"])
                V(lambda e: e.tensor_scalar(Y2[lo], cim[lo], -1.0, None, ALU.mult), r=["cim"], w=["Y2"])
                V(lambda e: e.tensor_scalar(Y2[hi], cre[hi], -1.0, None, ALU.mult), r=["cre"], w=["Y2"])
                for (idx, R2, I2, rn, inn) in ((33, AR2, AI2, "AR2", "AI2"), (34, PR2, PI2, "PR2", "PI2")):
                    for s_ in range(2):
                        V(lambda e, s_=s_, R2=R2, idx=idx: e.tensor_copy(R2[:, s_, :], PWr[:, idx, :]), r=["PWr"], w=[rn])
                    V(lambda e, I2=I2, idx=idx: e.tensor_scalar(I2[lo, 0, :], PWi[lo, idx, :], -1.0, None, ALU.mult), r=["PWi"], w=[inn])
                    V(lambda e, I2=I2, idx=idx: e.tensor_copy(I2[hi, 0, :], PWi[hi, idx, :]), r=["PWi"], w=[inn])
                    V(lambda e, I2=I2, idx=idx: e.tensor_copy(I2[lo, 1, :], PWi[lo, idx, :]), r=["PWi"], w=[inn])
                    V(lambda e, I2=I2, idx=idx: e.tensor_scalar(I2[hi, 1, :], PWi[hi, idx, :], -1.0, None, ALU.mult), r=["PWi"], w=[inn])
                S.barrier()
                ses1.close()
                big1 = sbs("big1", [128, 64, 8, 16], F32); big2 = sbs("big2", [128, 64, 8, 16], F32)
                E = [sbs("E%d" % t, [128, 64, 128], BF16) for t in range(4)]
                stage = [sbs("stage0", [128, 64, 128], BF16)] * 2
                for lv, idx in enumerate((33, 35, 36, 37, 38, 39, 40)):
                    V(lambda e, lv=lv, idx=idx: e.tensor_copy(TR[:, lv, :], PWr[:, idx, :]), r=["PWr"], w=["TR"])
                    V(lambda e, lv=lv, idx=idx: e.tensor_scalar(TI[lo, lv, 0, :], PWi[lo, idx, :], -1.0, None, ALU.mult), r=["PWi"], w=["TI"])
                    V(lambda e, lv=lv, idx=idx: e.tensor_copy(TI[hi, lv, 0, :], PWi[hi, idx, :]), r=["PWi"], w=["TI"])
                    V(lambda e, lv=lv, idx=idx: e.tensor_copy(TI[lo, lv, 1, :], PWi[lo, idx, :]), r=["PWi"], w=["TI"])
                    V(lambda e, lv=lv, idx=idx: e.tensor_scalar(TI[hi, lv, 1, :], PWi[hi, idx, :], -1.0, None, ALU.mult), r=["PWi"], w=["TI"])
                for t, (Xa, Xb, xan, xbn) in enumerate(((X1, X2, "X1", "X2"), (X1, X2, "X1", "X2"),
                                                        (Y1, Y2, "Y1", "Y2"), (Y1, Y2, "Y1", "Y2"))):
                    prv = PWr[:, 8 * t:8 * t + 8, :].rearrange("p r g -> p g r").unsqueeze(3).to_broadcast([128, 64, 8, 16])
                    piv = PWi[:, 8 * t:8 * t + 8, :].rearrange("p r g -> p g r").unsqueeze(3).to_broadcast([128, 64, 8, 16])
                    xa = Xa[:].unsqueeze(2).to_broadcast([128, 64, 8, 16])
                    xb = Xb[:].unsqueeze(2).to_broadcast([128, 64, 8, 16])
                    V(lambda e, prv=prv, xa=xa: e.tensor_tensor(out=big1[:], in0=prv, in1=xa, op=ALU.mult), r=["PWr", xan], w=["big1"])
                    V(lambda e, piv=piv, xb=xb: e.tensor_tensor(out=big2[:], in0=piv, in1=xb, op=ALU.mult), r=["PWi", xbn], w=["big2"])
                    V(lambda e, t=t: e.tensor_tensor(out=E[t][:].rearrange("p g (r c) -> p g r c", r=8), in0=big1[:], in1=big2[:],
                                                     op=ALU.add), r=["big1", "big2"], w=["E%d" % t])
                S.dma("sp", Om_d, E[2][:], reads=["E2"], sem_key="tabw")
                for which, (dst, sti) in enumerate(((RT_d, 0), (RTs_d, 1), (Toep_d, 0))):
                    st, stn = stage[sti], "stage0"
                    for gb in range(16):
                        pm, pmn = next_ps()
                        for gg in range(4):
                            g = 4 * gb + gg
                            if which == 0:
                                lhs, rhs, rd = E[0][:, g, :], ident[:], ["E0", "ident"]
                            elif which == 1:
                                lhs, rhs, rd = E[0][:, g, :], psw[:], ["E0", "psw"]
                            else:
                                lhs, rhs, rd = E[1][:, g, :], E[3][:, g, :], ["E1", "E3"]
                            S.op("pe", lambda e, pm=pm, gg=gg, lhs=lhs, rhs=rhs: e.matmul(
                                pm[:, gg * 128:(gg + 1) * 128], lhs, rhs, start=True, stop=True), reads=rd, writes=[pmn])
                        if which == 2:
                            V(lambda e, pm=pm: e.tensor_tensor(
                                out=big1[:, 0:4, :, :].rearrange("p g r c -> p g (r c)"), in0=pm[:, :].rearrange("p (g m) -> p g m", g=4),
                                in1=bmk[:].unsqueeze(1).to_broadcast([128, 4, 128]), op=ALU.mult), r=[pmn, "bmk"], w=["big1"])
                            for gg in range(4):
                                g = 4 * gb + gg
                                V(lambda e, gg=gg, g=g, st=st: e.scalar_tensor_tensor(
                                    out=st[:, g, :], in0=identf[:], scalar=ddt[:, g:g + 1],
                                    in1=big1[:, gg, :, :].rearrange("p r c -> p (r c)"), op0=ALU.mult, op1=ALU.add),
                                    r=["identf", "ddt", "big1"], w=[stn])
                        else:
                            A_(lambda e, pm=pm, gb=gb, st=st: e.copy(out=st[:, 4 * gb:4 * gb + 4, :],
                                                                     in_=pm[:, :].rearrange("p (g m) -> p g m", g=4)), r=[pmn], w=[stn])
                    S.dma("sp", dst, st[:], reads=[stn], sem_key="tabw")
                S.barrier()

        if WCONV:
            for wid in (0, 1, 2, 3, 4, 5):
                sv = SRCV[wid]
                dv, dn = BFV[wid]
                ncols_ = sv.shape[2]
                for c0 in range(0, ncols_, 512):
                    S.dma("pool", dv[:, :, c0:c0 + 512], sv[:, :, c0:c0 + 512], writes=[dn], sem_key=dn)
        if ENABLE_SSM:
            ssm_setup()
        selR = sb("selR", [128, 64, 128], BF16)
        x_tok = sb("x_tok", [128, 4, D], F32)
        h_tok = sb("h_tok", [128, 4, D], BF16)
        ssq = sb("ssq", [128, 8], F32)
        rstd = sb("rstd", [128, 8], F32)
        hT = sb("hT", [128, 8, 512], BF16)
        kT2 = sb("kT2", [128, 4, 640], BF16)
        vpad = sb("vpad", [128, 5, 4, 2, 128], BF16)
        kv_tok = sb("kv_tok", [128, 4, 512], F32)
        gyT = sb("gyT", [128, 8, 512], BF16)
        sm = sb("sm", [128, 32], F32)
        msc = sb("msc", [16, 128], F32); msn = sb("msn", [16, 4], F32)
        sink_s = sb("sink_s", [16, 4], F32); nsink_s = sb("nsink_s", [16, 4], F32)
        vnf = sb("vnf", [4, 256], F32); vnb = sb("vnb", [4, 256], BF16)
        vnf2 = sb("vnf2", [4, 256], F32); vnb2 = sb("vnb2", [4, 256], BF16)
        wslot = [sb("wslot%d" % i, [128, 8192], BF16) for i in range(2)]
        V(lambda e: e.memset(vpad[:], 0.0), w=["vpad"])
        S.dma("sp", msc[:], msc_in, writes=["msc"]); S.dma("sp", msn[:], msn_in, writes=["msn"])
        S.dma("sp", sink_s[:], sink_s_in, writes=["sink_s"])
        V(lambda e: e.tensor_scalar(nsink_s[:], sink_s[:], -1.0, None, ALU.mult), r=["sink_s"], w=["nsink_s"])
        for hh_ in range(2):
            S.dma("pool", selR[:, 32 * hh_:32 * hh_ + 32, :], selc[:, 32 * hh_:32 * hh_ + 32, :], writes=["selR"], sem_key="selR")

        work = sb("work", [128, 32768], BF16)
        KB = 512
        uT = work[:, 0:8 * KB].rearrange("p (k m) -> p k m", k=8)
        Up = work[:, 8 * KB:16 * KB].rearrange("p (g n) -> p g n", g=64)
        SSx = work[:, 16 * KB:48 * KB].bitcast(F32).rearrange("p (n s g) -> p n s g", n=64, s=2)
        Hp = work[:, 48 * KB:56 * KB].rearrange("p (g n) -> p g n", g=64)
        Yp = work[:, 56 * KB:64 * KB].rearrange("p (g n) -> p g n", g=64)
        qT = work[:, 0:8 * KB].rearrange("p (k m) -> p k m", k=8)
        oT = work[:, 8 * KB:16 * KB].rearrange("p (k m) -> p k m", k=8)
        mA = work[:, 16 * KB:24 * KB].rearrange("p (k m) -> p k m", k=8)
        mB = work[:, 24 * KB:32 * KB].rearrange("p (k m) -> p k m", k=8)
        aT = work[:, 32 * KB:48 * KB].rearrange("p (k m) -> p k m", k=16)
        rtmp = work[:, 56 * KB:58 * KB].bitcast(F32)
        sc_ = [work[:, 48 * KB:52 * KB].bitcast(F32).rearrange("p (h s) -> p h s", h=4),
               work[:, 58 * KB:62 * KB].bitcast(F32).rearrange("p (h s) -> p h s", h=4)]
        Pb_ = [work[:, 52 * KB:54 * KB].rearrange("p (h s) -> p h s", h=4),
               work[:, 62 * KB:64 * KB].rearrange("p (h s) -> p h s", h=4)]
        PT_ = [work[:, 54 * KB:56 * KB].rearrange("p (h s) -> p h s", h=8),
               work[:, 56 * KB:58 * KB].rearrange("p (h s) -> p h s", h=8)]
        sm_ = [sm, sb("sm2", [128, 32], F32)]
        ytmp = sb("ytmp", [128, 512], F32)
        ztmp = sb("ztmp", [128, 512], F32)
        junk = ztmp[:].bitcast(BF16)
        stmp = sb("stmp", [128, 128], F32)
        WORK_NAMES = ["HpYp", "uT", "Up", "SSx", "Hp", "Yp", "qT", "oT", "mA", "mB", "aT", "sc0", "sc1", "Pb0", "Pb1", "PT0", "PT1", "rtmp"]

        def phase_barrier():
            S.barrier()
            for n in WORK_NAMES:
                S.wr.pop(n, None)
                S.rd.pop(n, None)


        def piped(tag):
            def deco(fn):
                def wrapped(*a, **k):
                    old = S.pipe
                    S.pipe = tag in PIPE
                    try:
                        return fn(*a, **k)
                    finally:
                        S.pipe = old
                return wrapped
            return deco

        def rms_stats(xt, xname, nsub, psz):
            for j in range(nsub):
                A_(lambda e, j=j: e.activation(out=junk[:psz, :], in_=xt[:psz, j, :], func=AF.Square,
                                               accum_out=ssq[:psz, j:j + 1]), r=[xname], w=["ztmp", "ssq"])
            V(lambda e: e.tensor_scalar(rstd[:psz, :nsub], ssq[:psz, :nsub], 1.0 / D, EPS, ALU.mult, ALU.add), r=["ssq"], w=["rstd"])
            A_(lambda e: e.activation(out=rstd[:psz, :nsub], in_=rstd[:psz, :nsub], func=AF.Sqrt), r=["rstd"], w=["rstd"])
            V(lambda e: e.reciprocal(rstd[:psz, :nsub], rstd[:psz, :nsub]), r=["rstd"], w=["rstd"])

        @piped("H")
        def rmsnorm_to_T(xt, xname, gt, gname, nsub, psz):
            rms_stats(xt, xname, nsub, psz)
            for j in range(nsub):
                V(lambda e, j=j: e.scalar_tensor_tensor(out=h_tok[:psz, j, :], in0=xt[:psz, j, :], scalar=rstd[:psz, j:j + 1],
                                                        in1=gt[:psz, :], op0=ALU.mult, op1=ALU.mult),
                  r=[xname, "rstd", gname], w=["h_tok"])
            for j in range(nsub):
                pt, pn = next_pt()
                for k in range(8):
                    S.op("pe", lambda e, k=k, j=j, pt=pt: e.transpose(
                        pt[:, k, :psz], h_tok[:psz, j, k * 128:(k + 1) * 128], ident[:psz, :psz]),
                        reads=["h_tok", "ident"], writes=[pn], inc=(k == 7), mode=("T" if psz == 128 else "T64"))
                A_(lambda e, j=j, pt=pt: e.copy(out=hT[:, :, j * 128:j * 128 + psz], in_=pt[:, :, :psz]), r=[pn], w=["hT"])

        def fm_proj(wv, wn, m, src, sname, nt, nk=8):
            pm, pmn = next_ps()
            S.pipe = "A" in PIPE
            for k in range(nk):
                S.op("pe", lambda e, k=k: e.matmul(pm[:, :nt], wv[:, k, m * 128:(m + 1) * 128], src[:, k, :nt],
                                                   start=(k == 0), stop=(k == nk - 1)),
                     reads=[sname, wn], writes=[pmn], inc=(k == nk - 1))
            S.pipe = False
            return pm, pmn

        @piped("G")
        def kv_token_major(wkv, wkv_n, j, psz, blk):
            pm, pmn = next_ps()
            for k in range(8):
                S.op("pe", lambda e, k=k: e.matmul(pm[:psz, :], hT[:, k, j * 128:j * 128 + psz], wkv[:, k, 0:512],
                                                   start=(k == 0), stop=(k == 7)),
                     reads=["hT", wkv_n], writes=[pmn], inc=(k == 7), mode=("full" if psz == 128 else "m64"))
            V(lambda e: e.tensor_copy(kv_tok[:psz, j, :], pm[:psz, :]), r=[pmn], w=["kv_tok"])
            if blk is not None:
                vsrc = kv_tok[:psz, j, 256:512].rearrange("p (h d) -> p h d", h=4)
                A_(lambda e: e.copy(out=vpad[:psz, blk, :, 0, 0:64], in_=vsrc), r=["kv_tok"], w=["vpad"])
                A_(lambda e: e.copy(out=vpad[:psz, blk, :, 1, 64:128], in_=vsrc), r=["kv_tok"], w=["vpad"])

        @piped("G")
        def k_feature_major(wkd, wkd_n, nt, col0):
            for h in range(4):
                pm, pmn = fm_proj(wkd, wkd_n, h, hT, "hT", nt)
                A_(lambda e, h=h, pm=pm: e.copy(out=kT2[:, h, col0:col0 + nt], in_=pm[:, :nt]), r=[pmn], w=["kT2"])

        @piped("F")
        def attention_s1(j, kvh, mask_t, mask_n, par):
            sc, Pb, PT, sm = sc_[par], Pb_[par], PT_[par], sm_[par]
            scn, Pbn, PTn, smn = "sc%d" % par, "Pb%d" % par, "PT%d" % par, "sm%d" % par
            ptw = [PTn] + (["rtmp"] if par == 1 else [])
            pss = []
            for i2 in range(2):
                pm, pmn = next_ps()
                pss.append((pm, pmn))
                for a in range(2):
                    tile, base = 2 * kvh + a, 64 * i2
                    S.op("pe", lambda e, pm=pm, a=a, tile=tile, base=base: e.matmul(
                        pm[:, a * 256:(a + 1) * 256], qT[base:base + 64, tile, j * 128:(j + 1) * 128],
                        kT2[base:base + 64, kvh, j * 128:(j + 2) * 128], start=True, stop=True),
                        reads=["qT", "kT2"], writes=[pmn], inc=(a == 1), mode="k64")
            for i2 in range(2):
                pm, pmn = pss[i2]
                V(lambda e, pm=pm, i2=i2: e.tensor_tensor(
                    out=sc[:, i2:4:2, :], in0=pm[:, :].rearrange("p (h s) -> p h s", h=2),
                    in1=mask_t[:].unsqueeze(1).to_broadcast([128, 2, 256]), op=ALU.add), r=[pmn, mask_n], w=[scn])
            negm, rs, es_, den = sm[:, 0:4], sm[:, 4:8], sm[:, 8:12], sm[:, 12:16]
            V(lambda e: e.reduce_max(out=negm, in_=sc[:], axis=AX.X, negate=True), r=[scn], w=[smn])
            V(lambda e: e.tensor_tensor(out=negm, in0=negm, in1=nsink_bc[:, 4 * kvh:4 * kvh + 4], op=ALU.min),
              r=[smn, "nsink_bc"], w=[smn])
            for i in range(4):
                A_(lambda e, i=i: e.activation(out=Pb[:, i, :], in_=sc[:, i, :], func=AF.Exp,
                                               bias=negm[:, i:i + 1], accum_out=rs[:, i:i + 1]), r=[scn, smn], w=[Pbn, smn])
            V(lambda e: e.tensor_tensor(out=es_, in0=negm, in1=sink_bc[:, 4 * kvh:4 * kvh + 4], op=ALU.add),
              r=[smn, "sink_bc"], w=[smn])
            A_(lambda e: e.activation(out=es_, in_=es_, func=AF.Exp), r=[smn], w=[smn])
            V(lambda e: e.tensor_tensor(out=den, in0=rs, in1=es_, op=ALU.add), r=[smn], w=[smn])
            V(lambda e: e.reciprocal(den, den), r=[smn], w=[smn])
            V(lambda e: e.tensor_tensor(out=Pb[:], in0=Pb[:], in1=den.unsqueeze(2).to_broadcast([128, 4, 256]), op=ALU.mult),
              r=[Pbn, smn], w=[Pbn])

        @piped("F")
        def attention_s2(j, kvh, par):
            sc, Pb, PT, sm = sc_[par], Pb_[par], PT_[par], sm_[par]
            scn, Pbn, PTn, smn = "sc%d" % par, "Pb%d" % par, "PT%d" % par, "sm%d" % par
            ptw = [PTn] + (["rtmp"] if par == 1 else [])
            pt, pn = next_pt()
            for i in range(4):
                for blk in range(2):
                    S.op("pe", lambda e, i=i, blk=blk: e.transpose(pt[:, 2 * i + blk, :], Pb[:, i, blk * 128:(blk + 1) * 128], ident[:]),
                         reads=[Pbn, "ident"], writes=[pn], inc=(i == 3 and blk == 1), mode="T")
            A_(lambda e: e.copy(out=PT[:], in_=pt[:]), r=[pn], w=ptw)
            for a in range(2):
                pm, pmn = next_ps()
                n = 0
                for i2 in range(2):
                    for blk in range(2):
                        S.op("pe", lambda e, pm=pm, i2=i2, blk=blk, n=n: e.matmul(
                            pm[:, 0:128], vpad[:, j + blk, kvh, i2, :], PT[:, 2 * (2 * a + i2) + blk, :],
                            start=(n == 0), stop=(n == 3)), reads=["vpad"] + ptw, writes=[pmn], inc=(n == 3))
                        n += 1
                V(lambda e, pm=pm, a=a: e.tensor_copy(oT[:, 2 * kvh + a, j * 128:(j + 1) * 128], pm[:, 0:128]), r=[pmn], w=["oT"])

        def ssm_u(nt, bf=True):
            wu, wu_n = load_w(w_in_v, 8, 1536, 1024, bf=bf)
            for m in range(8):
                pm, pmn = fm_proj(wu, wu_n, m, hT, "hT", nt)
                A_(lambda e, m=m, pm=pm: e.copy(out=uT[:, m, :nt], in_=pm[:, :nt]), r=[pmn], w=["uT"])

        @piped("B")
        def ssm_relayout(ncols, r_list, colsl):
            for q in range(8):
                pm, pmn = next_ps()
                for gl_ in range(8):
                    for ri, r in enumerate(r_list):
                        S.op("pe", lambda e, pm=pm, gl_=gl_, r=r, q=q: e.matmul(
                            pm[:, gl_ * 64:gl_ * 64 + ncols], selR[:, gl_ * 8 + r, :], uT[:, q, colsl(r)],
                            start=(ri == 0), stop=(ri == len(r_list) - 1)),
                            reads=["selR", "uT"], writes=[pmn], inc=(gl_ == 7 and ri == len(r_list) - 1))
                A_(lambda e, pm=pm, q=q: e.copy(out=Up[:, 8 * q:8 * q + 8, :ncols],
                                               in_=pm[:, :].rearrange("p (g n) -> p g n", g=8)[:, :, :ncols]), r=[pmn], w=["Up"])


        @piped("B")
        def ssm_relayout4():
            Upv = Up[:, :, :].rearrange("p (q i l) n -> p q i l n", q=8, i=4)
            for qh in range(2):
                banks = [next_ps() for _ in range(4)]
                for ql in range(4):
                    q = 4 * qh + ql
                    for l in range(2):
                        for r in range(8):
                            for i in range(4):
                                gl_ = 2 * i + l
                                pm, pmn = banks[i]
                                col = (ql * 2 + l) * 64
                                last = (ql == 3 and l == 1 and r == 7)
                                S.op("pe", lambda e, pm=pm, i=i, gl_=gl_, r=r, q=q, col=col: e.matmul(
                                    pm[:, col:col + 64], selR[32 * i:32 * i + 32, gl_ * 8 + r, :],
                                    uT[32 * i:32 * i + 32, q, r:512:8], start=(r == 0), stop=(r == 7),
                                    tile_position=(32 * i, 0)),
                                    reads=["selR", "uT"], writes=[pmn], inc=last, mode="k32")
                for i in range(4):
                    pm, pmn = banks[i]
                    A_(lambda e, pm=pm, i=i, qh=qh: e.copy(
                        out=Upv[:, 4 * qh:4 * qh + 4, i, :, :],
                        in_=pm[:, :].rearrange("p (q l n) -> p q l n", q=4, l=2)), r=[pmn], w=["Up"])

        @piped("C")
        def ssm_S(ncols):
            for s_, src in enumerate((RT_d, RTs_d)):
                tab, tabn = load_tab(src)
                gpb = 512 // max(ncols, 1) if ncols >= 64 else 8
                gpb = 8
                for gb in range(64 // gpb):
                    pm, pmn = next_ps()
                    for gg in range(gpb):
                        g = gb * gpb + gg
                        S.op("pe", lambda e, pm=pm, gg=gg, g=g, tab=tab: e.matmul(
                            pm[:, gg * 64:gg * 64 + ncols], tab[:, g, :], Up[:, g, :ncols], start=True, stop=True),
                            reads=[tabn, "Up"], writes=[pmn], inc=(gg == gpb - 1))
                    A_(lambda e, pm=pm, gb=gb, s_=s_: e.copy(
                        out=SSx[:, 0:ncols, s_, gb * gpb:(gb + 1) * gpb],
                        in_=pm[:, :].rearrange("p (g n) -> p n g", g=8)[:, :ncols, :]), r=[pmn], w=["SSx"])

        def ssm_recur(ncols, store_hp):
            for n in range(ncols):
                if store_hp:
                    V(lambda e, n=n: e.tensor_copy(Hp[:, :, n], X[:, 0, :]), r=["X"], w=["Hp"])
                V(lambda e: e.tensor_tensor(out=t1[:], in0=AR2[:], in1=X[:, 0:2, :], op=ALU.mult), r=["AR2", "X"], w=["t1"])
                V(lambda e: e.tensor_tensor(out=t2[:], in0=AI2[:], in1=X[:, 1::-1, :], op=ALU.mult), r=["AI2", "X"], w=["t2"])
                V(lambda e: e.tensor_tensor(out=t1[:], in0=t1[:], in1=t2[:], op=ALU.add), r=["t1", "t2"], w=["t1"])
                V(lambda e, n=n: e.tensor_tensor(out=X[:, 0:2, :], in0=t1[:], in1=SSx[:, n, :, :], op=ALU.add),
                  r=["t1", "SSx"], w=["X"])


        def ssm_reduce():
            bufA = work[:, 16 * KB:48 * KB].bitcast(F32)
            bufB = work[:, 48 * KB:64 * KB].bitcast(F32)
            tmp = kv_tok[:].rearrange("p a b -> p (a b)")
            src, srcn, dst, dstn, n = bufA, "SSx", bufB, "HpYp", 64
            for lv in range(6):
                m = n // 2
                sv = src[:, 0:n * 128].rearrange("p (m e s g) -> p m e s g", e=2, s=2, g=64)
                ev, od = sv[:, :, 0], sv[:, :, 1]
                dv = dst[:, 0:m * 128].rearrange("p (m s g) -> p m s g", s=2, g=64)
                tv = tmp[:, 0:m * 64].rearrange("p (m g) -> p m g", g=64)
                V(lambda e, dv=dv, ev=ev, lv=lv, m=m: e.tensor_tensor(
                    out=dv, in0=ev, in1=TR[:, lv, :].unsqueeze(1).unsqueeze(1).to_broadcast([128, m, 2, 64]), op=ALU.mult), r=[srcn, "TR"], w=[dstn])
                for s_ in range(2):
                    V(lambda e, tv=tv, ev=ev, lv=lv, m=m, s_=s_: e.tensor_tensor(
                        out=tv, in0=ev[:, :, 1 - s_, :], in1=TI[:, lv, s_, :].unsqueeze(1).to_broadcast([128, m, 64]), op=ALU.mult),
                        r=[srcn, "TI"], w=["kv_tok"])
                    V(lambda e, tv=tv, dv=dv, s_=s_: e.tensor_tensor(out=dv[:, :, s_, :], in0=dv[:, :, s_, :], in1=tv, op=ALU.add),
                      r=[dstn, "kv_tok"], w=[dstn])
                V(lambda e, dv=dv, od=od: e.tensor_tensor(out=dv, in0=dv, in1=od, op=ALU.add), r=[dstn, srcn], w=[dstn])
                src, srcn, dst, dstn, n = dst, dstn, src, srcn, m
            fin = src[:, 0:128].rearrange("p (s g) -> p s g", s=2)
            V(lambda e: e.tensor_tensor(out=t1[:], in0=TR[:, 6, :].unsqueeze(1).to_broadcast([128, 2, 64]), in1=X[:, 0:2, :], op=ALU.mult), r=["TR", "X"], w=["t1"])
            V(lambda e: e.tensor_tensor(out=t2[:], in0=TI[:, 6], in1=X[:, 1::-1, :], op=ALU.mult), r=["TI", "X"], w=["t2"])
            V(lambda e: e.tensor_tensor(out=t1[:], in0=t1[:], in1=t2[:], op=ALU.add), r=["t1", "t2"], w=["t1"])
            V(lambda e, fin=fin: e.tensor_tensor(out=X[:, 0:2, :], in0=t1[:], in1=fin, op=ALU.add), r=["t1", srcn], w=["X"])

        @piped("D")
        def ssm_Y(ncols):
            tT, tTn = load_tab(Toep_d)
            tO, tOn = load_tab(Om_d)
            for gb in range(8):
                pm, pmn = next_ps()
                for gg in range(8):
                    g = gb * 8 + gg
                    S.op("pe", lambda e, pm=pm, gg=gg, g=g: e.matmul(
                        pm[:, gg * 64:gg * 64 + ncols], tT[:, g, :], Up[:, g, :ncols], start=True, stop=False),
                        reads=[tTn, "Up"], writes=[pmn], inc=False)
                    S.op("pe", lambda e, pm=pm, gg=gg, g=g: e.matmul(
                        pm[:, gg * 64:gg * 64 + ncols], tO[:, g, :], Hp[:, g, :ncols], start=False, stop=True),
                        reads=[tOn, "Hp"], writes=[pmn], inc=(gg == 7))
                A_(lambda e, pm=pm, gb=gb: e.copy(out=Yp[:, 8 * gb:8 * gb + 8, :ncols],
                                                 in_=pm[:, :].rearrange("p (g n) -> p g n", g=8)[:, :, :ncols]), r=[pmn], w=["Yp"])

        @piped("E")
        def ssm_back(nt, ncols, r_list):
            tS, tSn = load_tab(selTc)
            nr = len(r_list)
            for q in range(8):
                pm, pmn = next_ps()
                for ri, r in enumerate(r_list):
                    for gl_ in range(8):
                        S.op("pe", lambda e, pm=pm, ri=ri, r=r, gl_=gl_, q=q: e.matmul(
                            pm[:, ri * ncols:(ri + 1) * ncols], tS[:, gl_ * 8 + r, :], Yp[:, 8 * q + gl_, :ncols],
                            start=(gl_ == 0), stop=(gl_ == 7)), reads=[tSn, "Yp"], writes=[pmn],
                            inc=(gl_ == 7 and ri == nr - 1))
                pv = pm[:, 0:nr * ncols].rearrange("p (r n) -> p n r", r=nr)
                uv = uT[:, q, :nt].rearrange("p (n r) -> p n r", r=nr)
                yv = ytmp[:, :nt].rearrange("p (n r) -> p n r", r=nr)
                V(lambda e, pv=pv, yv=yv: e.tensor_copy(yv, pv), r=[pmn], w=["ytmp"])
                V(lambda e: e.tensor_tensor(out=ztmp[:, :nt], in0=ytmp[:, :nt], in1=ytmp[:, :nt], op=ALU.mult), r=["ytmp"], w=["ztmp"])
                V(lambda e: e.tensor_scalar(ztmp[:, :nt], ztmp[:, :nt], 0.044715, 1.0, ALU.mult, ALU.add), r=["ztmp"], w=["ztmp"])
                V(lambda e: e.tensor_tensor(out=ztmp[:, :nt], in0=ztmp[:, :nt], in1=ytmp[:, :nt], op=ALU.mult), r=["ztmp", "ytmp"], w=["ztmp"])
                A_(lambda e: e.activation(out=ztmp[:, :nt], in_=ztmp[:, :nt], func=AF.Sigmoid, scale=1.5957691216057308),
                   r=["ztmp"], w=["ztmp"])
                V(lambda e, q=q: e.tensor_tensor(out=gyT[:, q, :nt], in0=ztmp[:, :nt], in1=ytmp[:, :nt], op=ALU.mult),
                  r=["ztmp", "ytmp"], w=["gyT"])

        def state_out(src_ap, src_names, ncols, dst):
            for c0 in range(0, ncols, 128):
                cw = min(128, ncols - c0)
                pm, pmn = next_ps()
                S.op("pe", lambda e, pm=pm, c0=c0, cw=cw: e.transpose(pm[:cw, 0:128], src_ap[:, c0:c0 + cw], identf[:]),
                     reads=list(src_names) + ["identf"], writes=[pmn], mode="Tf32")
                V(lambda e, pm=pm, cw=cw: e.tensor_copy(stmp[:cw, 0:128], pm[:cw, 0:128]), r=[pmn], w=["stmp"])
                S.dma("sp", dst[c0:c0 + cw, :], stmp[:cw, 0:128], reads=["stmp"], sem_key="o_st")


        @piped("I")
        def sample_attention():
            SA = work[:, 32 * KB:48 * KB]
            qs = SA[0:64, 0:1024].rearrange("p (b k i t) -> p b k i t", b=16, k=4, i=4)
            qsv = SA[0:64, 0:1024].rearrange("p (b k i t) -> p (k i) b t", b=16, k=4, i=4)
            ksT = SA[0:64, 1024:1280].rearrange("p (h t) -> p h t", h=4)
            SAB = []
            for par in range(2):
                o = 1280 + par * 2048
                SAB.append((SA[0:16, o:o + 1024].bitcast(F32).rearrange("p (h s) -> p h s", h=4),
                            SA[0:16, o + 1024:o + 1056].bitcast(F32).rearrange("p (h s) -> p h s", h=4),
                            SA[0:16, o + 1056:o + 1568].rearrange("p (h s) -> p h s", h=4),
                            SA[0:16, o + 1568:o + 1584].rearrange("p (h s) -> p h s", h=4),
                            SA[:, o + 1584:o + 1648].rearrange("p (h s) -> p h s", h=4),
                            SA[0:4, o + 1648:o + 1712].rearrange("p (h s) -> p h s", h=4),
                            SA[0:16, o + 1712:o + 1968],
                            SA[0:16, o + 1968:o + 2032].bitcast(F32)))
            vn_ = [(vnf, vnb), (vnf2, vnb2)]
            wq, wq_n = load_w(w_in_v, 8, 0, 1024)
            for hb in range(2):
                pm, pmn = next_ps()
                for hh_ in range(8):
                    h = hb * 8 + hh_
                    for k in range(8):
                        S.op("pe", lambda e, pm=pm, hh_=hh_, h=h, k=k: e.matmul(
                            pm[:64, hh_ * 64:(hh_ + 1) * 64], wq[:, k, h * 64:(h + 1) * 64], hT[:, k, :64],
                            start=(k == 0), stop=(k == 7)), reads=[wq_n, "hT"], writes=[pmn], inc=(hh_ == 7 and k == 7), mode="m64")
                A_(lambda e, pm=pm, hb=hb: e.activation(out=qsv[:, hb * 8:(hb + 1) * 8, :, :],
                                                       in_=pm[:64, :].rearrange("p (h b t) -> p h b t", h=8, b=16), func=AF.Copy, scale=0.125),
                   r=[pmn], w=["qs"])
            wk, wk_n = load_w(w_in_v, 8, 1024, 256)
            pm, pmn = next_ps()
            for kvh in range(4):
                for k in range(8):
                    S.op("pe", lambda e, pm=pm, kvh=kvh, k=k: e.matmul(
                        pm[:64, kvh * 64:(kvh + 1) * 64], wk[:, k, kvh * 64:(kvh + 1) * 64], hT[:, k, :64],
                        start=(k == 0), stop=(k == 7)), reads=[wk_n, "hT"], writes=[pmn], inc=(kvh == 3 and k == 7), mode="m64")
            A_(lambda e, pm=pm: e.copy(out=ksT[:], in_=pm[:64, 0:256].rearrange("p (h t) -> p h t", h=4)), r=[pmn], w=["ksT"])
            slot, kvn = next_slot()
            kcv = slot[:, :].rearrange("p (w b c) -> p w b c", w=2, b=16)
            for w_, srcc in enumerate((ck, cv)):
                for hb in range(2):
                    S.dma("pool", kcv[:, w_, 8 * hb:8 * hb + 8, :], srcc[8 * hb:8 * hb + 8].rearrange("b s c -> s b c"), writes=[kvn])
            slot2, kTn = next_slot()
            kcT = slot2[0:64, :].rearrange("p (u s) -> p u s", u=64)
            for b in range(16):
                pt, pn = next_pt()
                for kvh in range(4):
                    S.op("pe", lambda e, pt=pt, kvh=kvh, b=b: e.transpose(pt[:64, kvh, :], kcv[:, 0, b, kvh * 64:(kvh + 1) * 64], ident[:]),
                         reads=[kvn, "ident"], writes=[pn], inc=(kvh == 3), mode="Tm64")
                A_(lambda e, pt=pt, b=b: e.copy(out=kcT[:, 4 * b:4 * b + 4, :], in_=pt[:64, 0:4, :]), r=[pn], w=[kTn])
            def sa_s1(b):
                    par = b % 2
                    scc, scn, Pc, Pn, PcT, PnT, ob, st_ = SAB[par]
                    vnfl, vnbl = vn_[par]
                    negm, m2, rs1, rs2, es_, den = (st_[:, 0:4], st_[:, 4:8], st_[:, 8:12], st_[:, 12:16], st_[:, 16:20], st_[:, 20:24])
                    N_ = lambda x: "%s_%d" % (x, par)
                    S.dma("sp", vnfl[:, :], kv_tok[4 * b:4 * b + 4, 0, 256:512], reads=["kv_tok"], writes=[N_("vnf")])
                    A_(lambda e: e.copy(out=vnbl[:, :], in_=vnfl[:, :]), r=[N_("vnf")], w=[N_("vnb")])
                    pmc, pcn = next_ps()
                    pmn_, pnn = next_ps()
                    for kvh in range(4):
                        lhs = qs[:, b, kvh, :, :].rearrange("p i t -> p (i t)")
                        S.op("pe", lambda e, pmc=pmc, kvh=kvh, lhs=lhs, b=b: e.matmul(
                            pmc[:16, kvh * 128:(kvh + 1) * 128], lhs, kcT[:, 4 * b + kvh, :], start=True, stop=True),
                            reads=["qs", kTn], writes=[pcn], inc=(kvh == 3), mode="k64m32")
                    for kvh in range(4):
                        lhs = qs[:, b, kvh, :, :].rearrange("p i t -> p (i t)")
                        S.op("pe", lambda e, pmn_=pmn_, kvh=kvh, lhs=lhs, b=b: e.matmul(
                            pmn_[:16, kvh * 4:(kvh + 1) * 4], lhs, ksT[:, kvh, 4 * b:4 * b + 4], start=True, stop=True),
                            reads=["qs", "ksT"], writes=[pnn], inc=(kvh == 3), mode="k64m32")
                    V(lambda e, pmc=pmc: e.tensor_tensor(out=scc, in0=pmc[:16, :].rearrange("p (h s) -> p h s", h=4),
                                                         in1=msc[:].unsqueeze(1).to_broadcast([16, 4, 128]), op=ALU.add), r=[pcn, "msc"], w=[N_("scc")])
                    V(lambda e, pmn_=pmn_: e.tensor_tensor(out=scn, in0=pmn_[:16, 0:16].rearrange("p (h s) -> p h s", h=4),
                                                           in1=msn[:].unsqueeze(1).to_broadcast([16, 4, 4]), op=ALU.add), r=[pnn, "msn"], w=[N_("scn")])
                    V(lambda e: e.reduce_max(out=negm, in_=scc, axis=AX.X, negate=True), r=[N_("scc")], w=[N_("st_")])
                    V(lambda e: e.reduce_max(out=m2, in_=scn, axis=AX.X, negate=True), r=[N_("scn")], w=[N_("st_")])
                    V(lambda e: e.tensor_tensor(out=negm, in0=negm, in1=m2, op=ALU.min), r=[N_("st_")], w=[N_("st_")])
                    V(lambda e: e.tensor_tensor(out=negm, in0=negm, in1=nsink_s[:], op=ALU.min), r=[N_("st_"), "nsink_s"], w=[N_("st_")])
                    for kvh in range(4):
                        A_(lambda e, kvh=kvh: e.activation(out=Pc[:, kvh, :], in_=scc[:, kvh, :], func=AF.Exp, bias=negm[:, kvh:kvh + 1],
                                                           accum_out=rs1[:, kvh:kvh + 1]), r=[N_("scc"), N_("st_")], w=[N_("Pc"), N_("st_")])
                        A_(lambda e, kvh=kvh: e.activation(out=Pn[:, kvh, :], in_=scn[:, kvh, :], func=AF.Exp, bias=negm[:, kvh:kvh + 1],
                                                           accum_out=rs2[:, kvh:kvh + 1]), r=[N_("scn"), N_("st_")], w=[N_("Pn"), N_("st_")])
                    V(lambda e: e.tensor_tensor(out=es_, in0=negm, in1=sink_s[:], op=ALU.add), r=[N_("st_"), "sink_s"], w=[N_("st_")])
                    A_(lambda e: e.activation(out=es_, in_=es_, func=AF.Exp), r=[N_("st_")], w=[N_("st_")])
                    V(lambda e: e.tensor_tensor(out=den, in0=rs1, in1=rs2, op=ALU.add), r=[N_("st_")], w=[N_("st_")])
                    V(lambda e: e.tensor_tensor(out=den, in0=den, in1=es_, op=ALU.add), r=[N_("st_")], w=[N_("st_")])
                    V(lambda e: e.reciprocal(den, den), r=[N_("st_")], w=[N_("st_")])
                    V(lambda e: e.tensor_tensor(out=Pc, in0=Pc, in1=den.unsqueeze(2).to_broadcast([16, 4, 128]), op=ALU.mult), r=[N_("Pc"), N_("st_")], w=[N_("Pc")])
                    V(lambda e: e.tensor_tensor(out=Pn, in0=Pn, in1=den.unsqueeze(2).to_broadcast([16, 4, 4]), op=ALU.mult), r=[N_("Pn"), N_("st_")], w=[N_("Pn")])
            def sa_s2(b):
                    par = b % 2
                    scc, scn, Pc, Pn, PcT, PnT, ob, st_ = SAB[par]
                    vnfl, vnbl = vn_[par]
                    negm, m2, rs1, rs2, es_, den = (st_[:, 0:4], st_[:, 4:8], st_[:, 8:12], st_[:, 12:16], st_[:, 16:20], st_[:, 20:24])
                    N_ = lambda x: "%s_%d" % (x, par)
                    pt, pn = next_pt()
                    for kvh in range(4):
                        S.op("pe", lambda e, pt=pt, kvh=kvh: e.transpose(pt[:, kvh, 0:16], Pc[:, kvh, :], ident[:16, :16]),
                             reads=[N_("Pc"), "ident"], writes=[pn], inc=False, mode="Tk32")
                        S.op("pe", lambda e, pt=pt, kvh=kvh: e.transpose(pt[0:4, 4 + kvh, 0:16], Pn[:, kvh, :], ident[:16, :16]),
                             reads=[N_("Pn"), "ident"], writes=[pn], inc=(kvh == 3), mode="Tk32b")
                    A_(lambda e, pt=pt: e.copy(out=PcT, in_=pt[:, 0:4, 0:16]), r=[pn], w=[N_("PcT")])
                    A_(lambda e, pt=pt: e.copy(out=PnT, in_=pt[0:4, 4:8, 0:16]), r=[pn], w=[N_("PnT")])
                    pmo, pon = next_ps()
                    for kvh in range(4):
                        S.op("pe", lambda e, pmo=pmo, kvh=kvh, b=b: e.matmul(
                            pmo[:16, kvh * 64:(kvh + 1) * 64], PcT[:, kvh, :], kcv[:, 1, b, kvh * 64:(kvh + 1) * 64], start=True, stop=False),
                            reads=[N_("PcT"), kvn], writes=[pon], inc=False, mode="m32")
                        S.op("pe", lambda e, pmo=pmo, kvh=kvh: e.matmul(
                            pmo[:16, kvh * 64:(kvh + 1) * 64], PnT[:, kvh, :], vnbl[:, kvh * 64:(kvh + 1) * 64], start=False, stop=True),
                            reads=[N_("PnT"), N_("vnb")], writes=[pon], inc=(kvh == 3), mode="k32m32")
                    A_(lambda e, pmo=pmo: e.copy(out=ob, in_=pmo[:16, 0:256]), r=[pon], w=[N_("ob")])
                    pt, pn = next_pt()
                    for a in range(2):
                        S.op("pe", lambda e, pt=pt, a=a: e.transpose(pt[:, a, 0:16], ob[:, a * 128:(a + 1) * 128], ident[:16, :16]),
                             reads=[N_("ob"), "ident"], writes=[pn], inc=(a == 1), mode="Tk32")
                    for a in range(2):
                        A_(lambda e, pt=pt, a=a, b=b: e.copy(out=oT[:, 4 * a:4 * a + 4, 4 * b:4 * b + 4],
                                                            in_=pt[:, a, 0:16].rearrange("p (i t) -> p i t", i=4)), r=[pn], w=["oT"])
            sa_s1(0)
            for b in range(16):
                if b + 1 < 16:
                    sa_s1(b + 1)
                sa_s2(b)
            S.barrier()

        if ENABLE_SSM:
            def pre_front(pt_i):
                S.dma("sp", x_tok[:, :, :], xpre[pt_i * 512:(pt_i + 1) * 512, :].rearrange("(j p) d -> p j d", p=128), writes=["x_tok"])
                rmsnorm_to_T(x_tok, "x_tok", gm, "gm", 4, 128)
                ssm_u(512, bf=False)
                (ssm_relayout4() if RELAYOUT4 else ssm_relayout(64, list(range(8)), lambda r: slice(r, 512, 8)))

            if NPRE > 0:
                pre_front(0)
                ssm_S(64)
            for pt_i in range(NPRE):
                if pt_i + 1 < NPRE:
                    pre_front(pt_i + 1)
                ssm_reduce()
                if pt_i + 1 < NPRE:
                    ssm_S(64)
            phase_barrier()

        S.dma("sp", x_tok[:, 0, :], xh, writes=["x_tok"])
        rmsnorm_to_T(x_tok, "x_tok", gm, "gm", 1, 128)
        wkv, wkv_n = load_w(w_in_v, 8, 1024, 512)
        kv_token_major(wkv, wkv_n, 0, 128, 0)
        wkd, wkd_n = load_w(w_in_v, 8, 1024, 256, dup_heads=True)
        k_feature_major(wkd, wkd_n, 128, 0)

        def front_b(kind, t0, nt, nsub, psz):
            wkv, wkv_n = load_w(w_in_v, 8, 1024, 512)
            for j in range(nsub):
                kv_token_major(wkv, wkv_n, j, psz, (j + 1) if kind == "p" else None)
            if kind == "p" and t0 + nt == TP:
                S.dma("sp", kp, kv_tok[:, 3, 0:256], reads=["kv_tok"], sem_key="o_kvp")
                S.dma("sp", vp, kv_tok[:, 3, 256:512], reads=["kv_tok"], sem_key="o_kvp")
            if kind == "s":
                for bb in range(16):
                    S.dma("sp", ks[bb, 124:128, :], kv_tok[4 * bb:4 * bb + 4, 0, 0:256], reads=["kv_tok"], sem_key="o_kvs")
                    S.dma("sp", vs[bb, 124:128, :], kv_tok[4 * bb:4 * bb + 4, 0, 256:512], reads=["kv_tok"], sem_key="o_kvs")
            if kind == "p":
                wkd, wkd_n = load_w(w_in_v, 8, 1024, 256, dup_heads=True)
                k_feature_major(wkd, wkd_n, nt, 128)
                wq, wq_n = load_w(w_in_v, 8, 0, 1024)
                for m in range(8):
                    pm, pmn = fm_proj(wq, wq_n, m, hT, "hT", nt)
                    A_(lambda e, m=m, pm=pm: e.activation(out=qT[:, m, :nt], in_=pm[:, :nt], func=AF.Copy, scale=0.125),
                       r=[pmn], w=["qT", "uT"])

        tiles = [("p", t * 512, 512) for t in range(int(os.environ.get("NPT", "4")))] + [("s", 0, 64)]
        for ti, (kind, t0, nt) in enumerate(tiles):
            nsub = max(1, nt // 128)
            psz = min(128, nt)
            xt, xname = x_tok, "x_tok"
            src = xp if kind == "p" else xs
            ydst = yp if kind == "p" else ys
            S.dma("sp", xt[:psz, :nsub, :], src[t0:t0 + nt, :].rearrange("(j p) d -> p j d", p=psz), writes=[xname])
            rmsnorm_to_T(xt, xname, gm, "gm", nsub, psz)

            if ENABLE_SSM:
                ssm_u(nt)
                if kind == "p":
                    (ssm_relayout4() if RELAYOUT4 else ssm_relayout(64, list(range(8)), lambda r: slice(r, 512, 8)))
                    ssm_S(64)
                    front_b(kind, t0, nt, nsub, psz)
                    ssm_recur(64, True)
                    ssm_Y(64)
                    ssm_back(nt, 64, list(range(8)))
                    if t0 + nt == TP:
                        state_out(X[:, 0, :], ["X"], 64, st_p)
                else:
                    ssm_relayout(16, [4, 5, 6, 7], lambda r: slice(r - 4, 64, 4))
                    ssm_S(16)
                    HH = SSx[:, 16:48, :, :].rearrange("p n s g -> p (n s g)")[:, 0:2048].rearrange("p (s b g) -> p s b g", s=2, b=16)
                    TT = SSx[:, 16:48, :, :].rearrange("p n s g -> p (n s g)")[:, 2048:4096].rearrange("p (s b g) -> p s b g", s=2, b=16)
                    H0 = SSx[:, 48:64, :, :].rearrange("p n s g -> p (n s g)")[:, 0:2048].rearrange("p (s b g) -> p s b g", s=2, b=16)
                    S.dma("sp", H0[:, 0], h0hh_in, writes=["H0"], sem_key="h0")
                    S.dma("sp", H0[:, 1], h0hs_in, writes=["H0"], sem_key="h0")
                    bcb = lambda t, s_: t[:, s_, :].unsqueeze(1).to_broadcast([128, 16, 64])
                    for s_ in range(2):
                        V(lambda e, s_=s_: e.tensor_tensor(out=HH[:, s_], in0=H0[:, s_], in1=bcb(PR2, s_), op=ALU.mult),
                          r=["H0", "PR2"], w=["HH"])
                        V(lambda e, s_=s_: e.tensor_tensor(out=TT[:, s_], in0=H0[:, 1 - s_], in1=bcb(PI2, s_), op=ALU.mult),
                          r=["H0", "PI2"], w=["TT"])
                    for s_ in range(2):
                        V(lambda e, s_=s_: e.tensor_tensor(out=HH[:, s_], in0=HH[:, s_], in1=TT[:, s_], op=ALU.add),
                          r=["HH", "TT"], w=["HH"])
                    A_(lambda e: e.copy(out=Hp[:, :, 0:16], in_=HH[:, 0].rearrange("p b g -> p g b")), r=["HH"], w=["Hp"])
                    ssm_Y(16)
                    ssm_back(nt, 16, [4, 5, 6, 7])
                    V(lambda e: e.tensor_tensor(out=H0[:, 0], in0=HH[:, 0], in1=bcb(AR2, 0), op=ALU.mult), r=["HH", "AR2"], w=["H0"])
                    V(lambda e: e.tensor_tensor(out=H0[:, 1], in0=HH[:, 1], in1=bcb(AI2, 0), op=ALU.mult), r=["HH", "AI2"], w=["H0"])
                    V(lambda e: e.tensor_tensor(out=H0[:, 0], in0=H0[:, 0], in1=H0[:, 1], op=ALU.add), r=["H0"], w=["H0"])
                    V(lambda e: e.tensor_tensor(out=H0[:, 0], in0=H0[:, 0], in1=SSx[:, 0:16, 0, :], op=ALU.add), r=["H0", "SSx"], w=["H0"])
                    state_out(H0[:, 0].rearrange("p b g -> p (b g)"), ["H0"], 1024, st_s)
                phase_barrier()

            if not (ENABLE_SSM and kind == "p"):
                front_b(kind, t0, nt, nsub, psz)
            if kind == "s" and not NO_ATTN:
                sample_attention()
            elif kind == "p" and not NO_ATTN:
                units = [(j, kvh) for j in range(4) for kvh in range(4)]
                def s1(u):
                    j, kvh = units[u]
                    first = (ti == 0 and j == 0)
                    attention_s1(j, kvh, m0_t if first else mA_t, "m0_t" if first else "mA_t", u % 2)
                s1(0)
                for u in range(16):
                    if u + 1 < 16:
                        s1(u + 1)
                    attention_s2(units[u][0], units[u][1], u % 2)
                A_(lambda e: e.copy(out=kT2[:, :, 0:128], in_=kT2[:, :, 512:640]), r=["kT2"], w=["kT2"])
                A_(lambda e: e.copy(out=vpad[:, 0], in_=vpad[:, 4]), r=["vpad"], w=["vpad"])
            else:
                V(lambda e: e.memset(oT[:, :, :nt], 0.0), w=["oT"])

            wg, wg_n = load_w(w_in_v, 8, 2560, 1024)
            for m in range(8):
                pm, pmn = fm_proj(wg, wg_n, m, hT, "hT", nt)
                A_(lambda e, m=m, pm=pm: e.activation(out=mA[:, m, :nt], in_=pm[:, :nt], func=AF.Sigmoid), r=[pmn], w=["mA"])
            if kind == "s":
                slot, wa_n = next_slot()
                wa = slot[:, :].rearrange("p (k m) -> p k m", k=8)
                wao_h = (wbf["w_ao"] if WCONV else w_ao).rearrange("(h d) m -> d h m", d=64)
                for a in range(2):
                    for kl in range(2):
                        S.dma("pool", wa[kl * 64:(kl + 1) * 64, 4 * a:4 * a + 4, :],
                              wao_h[:, 8 * a + 4 * kl:8 * a + 4 * kl + 4, :], reads=(["wb_w_ao"] if WCONV else []), writes=[wa_n])
            else:
                wa, wa_n = load_w(w_ao_v, 8, 0, 1024)
            for m in range(8):
                pm, pmn = fm_proj(wa, wa_n, m, oT, "oT", nt)
                V(lambda e, m=m, pm=pm: e.tensor_tensor(out=mA[:, m, :nt], in0=pm[:, :nt], in1=mA[:, m, :nt], op=ALU.mult),
                  r=[pmn, "mA"], w=["mA"])
            if ENABLE_SSM:
                wg, wg_n = load_w(w_in_v, 8, 3584, 1024)
                for m in range(8):
                    pm, pmn = fm_proj(wg, wg_n, m, hT, "hT", nt)
                    A_(lambda e, m=m, pm=pm: e.activation(out=mB[:, m, :nt], in_=pm[:, :nt], func=AF.Sigmoid), r=[pmn], w=["mB"])
                wb, wb_n = load_w(w_gl_v, 8, 1024, 1024)
                for m in range(8):
                    pm, pmn = fm_proj(wb, wb_n, m, gyT, "gyT", nt)
                    A_(lambda e, pm=pm: e.activation(out=rtmp[:, :nt], in_=pm[:, :nt], func=AF.Sigmoid), r=[pmn], w=["rtmp"])
                    V(lambda e, m=m: e.tensor_tensor(out=mB[:, m, :nt], in0=mB[:, m, :nt], in1=rtmp[:, :nt], op=ALU.mult),
                      r=["mB", "rtmp"], w=["mB"])
                wa2, wa2_n = load_w(w_gl_v, 8, 0, 1024)
                for m in range(8):
                    pm, pmn = fm_proj(wa2, wa2_n, m, gyT, "gyT", nt)
                    V(lambda e, m=m, pm=pm: e.tensor_tensor(out=rtmp[:, :nt], in0=pm[:, :nt], in1=mB[:, m, :nt], op=ALU.mult),
                      r=[pmn, "mB"], w=["rtmp"])
                    V(lambda e, m=m: e.tensor_tensor(out=mA[:, m, :nt], in0=mA[:, m, :nt], in1=rtmp[:, :nt], op=ALU.add),
                      r=["mA", "rtmp"], w=["mA"])
            wo, wo_n = load_w(w_o_v, 8, 0, 1024)
            S.pipe = "G" in PIPE
            for j in range(nsub):
                for cb in range(2):
                    pm, pmn = next_ps()
                    for k in range(8):
                        S.op("pe", lambda e, k=k, j=j, cb=cb, pm=pm: e.matmul(
                            pm[:psz, :], mA[:, k, j * 128:j * 128 + psz], wo[:, k, cb * 512:(cb + 1) * 512],
                            start=(k == 0), stop=(k == 7)), reads=["mA", wo_n], writes=[pmn], inc=(k == 7), mode=("full" if psz == 128 else "m64"))
                    V(lambda e, j=j, cb=cb, pm=pm: e.tensor_tensor(
                        out=xt[:psz, j, cb * 512:(cb + 1) * 512], in0=pm[:psz, :], in1=xt[:psz, j, cb * 512:(cb + 1) * 512],
                        op=ALU.add), r=[pmn, xname], w=[xname])

            S.pipe = False
            rmsnorm_to_T(xt, xname, gf, "gf", nsub, psz)
            for half in range(2):
                for cb in range(2):
                    wv, wn = load_w(w_up_v, 8, half * 2048 + cb * 1024, 1024)
                    for m in range(8):
                        pm, pmn = fm_proj(wv, wn, m, hT, "hT", nt)
                        f = cb * 8 + m
                        A_(lambda e, pm=pm: e.activation(out=rtmp[:, :nt], in_=pm[:, :nt], func=AF.Relu), r=[pmn], w=["rtmp"])
                        V(lambda e, f=f: e.tensor_tensor(out=aT[:, f, :nt], in0=rtmp[:, :nt], in1=rtmp[:, :nt], op=ALU.mult),
                          r=["rtmp"], w=["aT"])
                for cb in range(2):
                    wv, wn = load_w(w_down_v, 16, cb * 512, 512, k0=16 * half)
                    S.pipe = "G" in PIPE
                    for j in range(nsub):
                        pm, pmn = next_ps()
                        for k in range(16):
                            S.op("pe", lambda e, k=k, j=j, pm=pm, wv=wv: e.matmul(
                                pm[:psz, :], aT[:, k, j * 128:j * 128 + psz], wv[:, k, :],
                                start=(k == 0), stop=(k == 15)), reads=["aT", wn], writes=[pmn], inc=(k == 15), mode=("full" if psz == 128 else "m64"))
                        V(lambda e, j=j, cb=cb, pm=pm: e.tensor_tensor(
                            out=xt[:psz, j, cb * 512:(cb + 1) * 512], in0=pm[:psz, :],
                            in1=xt[:psz, j, cb * 512:(cb + 1) * 512], op=ALU.add), r=[pmn, xname], w=[xname])
            S.pipe = False
            rms_stats(xt, xname, nsub, psz)
            for j in range(nsub):
                V(lambda e, j=j: e.scalar_tensor_tensor(out=xt[:psz, j, :], in0=xt[:psz, j, :], scalar=rstd[:psz, j:j + 1],
                                                        in1=gl[:psz, :], op0=ALU.mult, op1=ALU.mult),
                  r=[xname, "rstd", "gl"], w=[xname])
            S.dma("sp", ydst[t0:t0 + nt, :].rearrange("(j p) d -> p j d", p=psz), xt[:psz, :nsub, :],
                  reads=[xname], sem_key="o_y")
            if ENABLE_SSM:
                phase_barrier()
        S.finish()
    return nc


_NC = None


def _masks(first_chunk):
    qi = np.arange(128)[:, None]
    si = np.arange(256)[None, :]
    diff = qi + 128 - si
    band = (diff >= 0) & (diff < 128)
    m_a = np.where(band, 0.0, NEG).astype(np.float32)
    m_0 = m_a.copy()
    if first_chunk:
        m_0[:, :128] = NEG
    return m_a, m_0


def _structural_constants():
    selc = np.zeros((128, 64, 128), np.float32)
    selT = np.zeros((128, 64, 128), np.float32)
    c = np.arange(16)
    for gl_ in range(8):
        for r in range(8):
            selc[gl_ * 16 + c, gl_ * 8 + r, r * 16 + c] = 1.0
            selT[r * 16 + c, gl_ * 8 + r, gl_ * 16 + c] = 1.0
    rr = np.arange(128) // 16
    bmask = (rr[:, None] <= rr[None, :]).astype(np.float32)
    r = np.arange(8)
    kvec = np.concatenate([7 - r, -r, r + 1, r, [1, 8, -4, 16, 32, 64, 128, 256, 512]]).astype(np.float32)
    t = np.repeat(np.arange(4)[None, :], 4, 0).reshape(16)
    msc = np.where(np.arange(128)[None, :] >= t[:, None] + 1, 0.0, NEG).astype(np.float32)
    msn = np.where(np.arange(4)[None, :] <= t[:, None], 0.0, NEG).astype(np.float32)
    return selc, selT, bmask, np.tile(kvec[None], (128, 1)), msc, msn


def kernel(x_prompt, x_sample, cache_k, cache_v, state_ssm_re, state_ssm_im, g_mix, w_in,
           attn_sinks, w_attn_o, ssm_lambda_re, ssm_lambda_im, ssm_log_dt, ssm_b_re, ssm_b_im,
           ssm_c_re, ssm_c_im, ssm_d, w_glu, w_out, g_ffn, w_up, w_down, g_final):
    global _NC
    f = lambda a: np.ascontiguousarray(np.asarray(a, dtype=np.float32))
    x_prompt, x_sample = f(x_prompt), f(x_sample)
    if _NC is None:
        _NC = build_nc()
    selc, selT, bmask, kvec, msc, msn = _structural_constants()
    sink_s = f(np.repeat(f(attn_sinks).reshape(4, 4).T, 4, axis=0))
    dup = lambda a: f(np.concatenate([a, a], 0))
    lamT_re, lamT_im = dup(f(ssm_lambda_re)[0].T), dup(f(ssm_lambda_im)[0].T)
    ldt = f(np.tile(f(ssm_log_dt)[0][None, :], (128, 1)))
    bT_re, bT_im = dup(f(ssm_b_re)[0].transpose(1, 0, 2)), dup(f(ssm_b_im)[0].transpose(1, 0, 2))
    cT_re, cT_im = dup(f(ssm_c_re)[0].transpose(2, 0, 1)), dup(f(ssm_c_im)[0].transpose(2, 0, 1))
    dT = f(f(ssm_d)[0].reshape(8, 128).T)
    dd = f(np.tile(f(ssm_d)[0].reshape(64, 16).T, (8, 1)))
    sre, sim_ = f(state_ssm_re)[0], f(state_ssm_im)[0]
    npre_rows = max(NPRE, 1) * 512
    in_maps = []
    for c in range(NCORES):
        b, k = c // 4, c % 4
        m_a, m_0 = _masks(k == 0)
        xh = x_prompt[b, k * TP - 128:k * TP] if k > 0 else np.zeros((128, D), np.float32)
        xpre = np.zeros((npre_rows, D), np.float32)
        if k > 0 and NPRE > 0:
            xpre[npre_rows - k * TP:] = x_prompt[b, 0:k * TP]
        hre = sre[16 * c:16 * c + 16].transpose(2, 0, 1)
        him = sim_[16 * c:16 * c + 16].transpose(2, 0, 1)
        in_maps.append({
            "xp": f(x_prompt[b, k * TP:(k + 1) * TP]), "xh": f(xh), "xpre": xpre,
            "xs": f(x_sample[16 * c:16 * c + 16].reshape(TS, D)),
            "ck": f(np.asarray(cache_k)[0, 16 * c:16 * c + 16].reshape(16, 128, 256)),
            "cv": f(np.asarray(cache_v)[0, 16 * c:16 * c + 16].reshape(16, 128, 256)),
            "g_mix": f(g_mix).reshape(1, D), "g_ffn": f(g_ffn).reshape(1, D), "g_fin": f(g_final).reshape(1, D),
            "sinks": f(attn_sinks).reshape(1, 16), "mask_a": m_a, "mask_0": m_0,
            "w_in": f(w_in)[0], "w_ao": f(w_attn_o)[0], "w_gl": f(w_glu)[0], "w_o": f(w_out)[0],
            "w_up": f(w_up)[0], "w_down": f(w_down)[0],
            "lamT_re": lamT_re, "lamT_im": lamT_im, "ldt": ldt, "bT_re": bT_re, "bT_im": bT_im,
            "cT_re": cT_re, "cT_im": cT_im, "dT": dT, "dd": dd, "kvec": kvec, "selc": selc, "selTc": selT, "bmask": bmask,
            "h0hh": f(np.concatenate([hre, him], 0)), "h0hs": f(np.concatenate([him, hre], 0)),
            "mask_sc": msc, "mask_sn": msn, "sink_sx": sink_s,
        })
    res = run_bass_kernel_spmd(_NC, in_maps, core_ids=list(range(NCORES)))
    R = res.results
    y_prompt = np.stack([np.concatenate([R[b * 4 + k]["yp"] for k in range(4)], 0) for b in range(2)], 0)
    y_sample = np.concatenate([R[c]["ys"].reshape(16, 4, D) for c in range(NCORES)], 0)
    k_prompt = np.stack([R[b * 4 + 3]["kp"].reshape(128, 4, 64) for b in range(2)], 0)[None]
    v_prompt = np.stack([R[b * 4 + 3]["vp"].reshape(128, 4, 64) for b in range(2)], 0)[None]
    k_sample = np.concatenate([R[c]["ks"].reshape(16, 128, 4, 64) for c in range(NCORES)], 0)[None]
    v_sample = np.concatenate([R[c]["vs"].reshape(16, 128, 4, 64) for c in range(NCORES)], 0)[None]
    ssm_re_p = np.stack([R[b * 4 + 3]["st_p"][:, 0:64] for b in range(2)], 0)[None]
    ssm_im_p = np.stack([R[b * 4 + 3]["st_p"][:, 64:128] for b in range(2)], 0)[None]
    ssm_re_s = np.concatenate([R[c]["st_s"].reshape(16, 64, 128)[:, :, 0:64] for c in range(NCORES)], 0)[None]
    ssm_im_s = np.concatenate([R[c]["st_s"].reshape(16, 64, 128)[:, :, 64:128] for c in range(NCORES)], 0)[None]
    asf = lambda a: np.ascontiguousarray(a, dtype=np.float32)
    return (asf(y_prompt), asf(y_sample), asf(k_prompt), asf(v_prompt), asf(ssm_re_p), asf(ssm_im_p),
            asf(k_sample), asf(v_sample), asf(ssm_re_s), asf(ssm_im_s))
```
